# Optimizing a Trainium2 kernel written in Bass

```python
import jax, jax.numpy as jnp
from jax import lax
import numpy as np

D_MODEL = 1024
BATCH = 8
SEQ = 4096
DEPTH = 1

CTX_LEN = 256
GRID_W = 64
ROPE_BASE = 10000.0
EPS = 1e-6
NEG_INF = -1e30

MLA_HEADS = 8
MLA_Q_RANK = 256
MLA_KV_RANK = 128
MLA_NOPE = 64
MLA_ROPE = 32
MLA_V = 64
MLA_SCALE = (MLA_NOPE + MLA_ROPE) ** -0.5

SWA_HEADS = 8
SWA_KV_HEADS = 2
SWA_HEAD_DIM = 64
WINDOW = 128
BLOCK = 128
SWA_SCALE = SWA_HEAD_DIM ** -0.5

D_FF = 2816
CONV_W = 3

W_QA = MLA_Q_RANK
W_KVA = MLA_KV_RANK
W_KR = MLA_ROPE
W_SQ = SWA_HEADS * SWA_HEAD_DIM
W_SKV = SWA_KV_HEADS * SWA_HEAD_DIM
W_GATE = 2 * D_MODEL
D_IN = W_QA + W_KVA + W_KR + W_SQ + 2 * W_SKV + W_GATE
SPLITS = (W_QA, W_QA + W_KVA, W_QA + W_KVA + W_KR, W_QA + W_KVA + W_KR + W_SQ,
          W_QA + W_KVA + W_KR + W_SQ + W_SKV, W_QA + W_KVA + W_KR + W_SQ + 2 * W_SKV)

kernel_name = 'hybrid_mla_swa_convffn_dit_block'


def rmsnorm(x, g):
    xf = x.astype(jnp.float32)
    y = xf * lax.rsqrt(jnp.mean(xf * xf, axis=-1, keepdims=True) + EPS)
    return (y * g.astype(jnp.float32)).astype(x.dtype)


def modulate(h, shift, scale):
    return h * (1 + scale[:, None, :]) + shift[:, None, :]


def rope_tables(s_len, rot_dim, dtype):
    rows = s_len // GRID_W
    row = jnp.repeat(jnp.arange(rows, dtype=jnp.float32), GRID_W)
    col = jnp.tile(jnp.arange(GRID_W, dtype=jnp.float32), rows)
    quarter = rot_dim // 4
    inv = ROPE_BASE ** (-jnp.arange(quarter, dtype=jnp.float32) / quarter)
    ar = row[:, None] * inv
    ac = col[:, None] * inv
    ang = jnp.concatenate([ar, ar, ac, ac], axis=-1)
    return jnp.cos(ang).astype(dtype), jnp.sin(ang).astype(dtype)


def apply_rope(x, cos, sin):
    q1, q2, q3, q4 = jnp.split(x, 4, axis=-1)
    rot = jnp.concatenate([-q2, q1, -q4, q3], axis=-1)
    return x * cos[None, :, None, :] + rot * sin[None, :, None, :]


def dense_attention(q, k, v, scale):
    s = jnp.einsum('bqhd,bkhd->bhqk', q, k).astype(jnp.float32) * scale
    p = jax.nn.softmax(s, axis=-1).astype(v.dtype)
    return jnp.einsum('bhqk,bkhd->bqhd', p, v)


def mla_heads(q_a, kv_a, k_r, q_norm_g, w_q_up, kv_norm_g, w_kv_up, rope):
    b, L, _ = q_a.shape
    q = (rmsnorm(q_a, q_norm_g) @ w_q_up).reshape(b, L, MLA_HEADS, MLA_NOPE + MLA_ROPE)
    q_nope, q_rope = q[..., :MLA_NOPE], q[..., MLA_NOPE:]
    kv = (rmsnorm(kv_a, kv_norm_g) @ w_kv_up).reshape(b, L, MLA_HEADS, MLA_NOPE + MLA_V)
    k_nope, v = kv[..., :MLA_NOPE], kv[..., MLA_NOPE:]
    k_rope = k_r.reshape(b, L, 1, MLA_ROPE)
    if rope is not None:
        cos, sin = rope
        q_rope = apply_rope(q_rope, cos, sin)
        k_rope = apply_rope(k_rope, cos, sin)
    k = jnp.concatenate([k_nope, jnp.broadcast_to(k_rope, (b, L, MLA_HEADS, MLA_ROPE))], axis=-1)
    q = jnp.concatenate([q_nope, q_rope], axis=-1)
    return q, k, v


def swa_heads(q_s, k_s, v_s, rope):
    b, L, _ = q_s.shape
    q = q_s.reshape(b, L, SWA_HEADS, SWA_HEAD_DIM)
    k = k_s.reshape(b, L, SWA_KV_HEADS, SWA_HEAD_DIM)
    v = v_s.reshape(b, L, SWA_KV_HEADS, SWA_HEAD_DIM)
    if rope is not None:
        cos, sin = rope
        q = apply_rope(q, cos, sin)
        k = apply_rope(k, cos, sin)
    return q, k, v


def mla_latent(q, k_all, v_all):
    b, s, h, dq = q.shape
    nb = s // BLOCK
    qb = q.reshape(b, nb, BLOCK, h, dq).transpose(1, 0, 2, 3, 4)
    out = lax.map(lambda qblk: dense_attention(qblk, k_all, v_all, MLA_SCALE), qb)
    return out.transpose(1, 0, 2, 3, 4).reshape(b, s, h * v_all.shape[-1])


def swa_latent(q, k, v, k_ctx, v_ctx, sink):
    b, s, hq, d = q.shape
    hkv = k.shape[2]
    g = hq // hkv
    nb = s // BLOCK
    c_len = k_ctx.shape[1]
    qb = q.reshape(b, nb, BLOCK, hkv, g, d)

    def band(t):
        tp = jnp.pad(t, ((0, 0), (BLOCK, BLOCK), (0, 0), (0, 0))).reshape(b, nb + 2, BLOCK, hkv, d)
        return jnp.concatenate([tp[:, :-2], tp[:, 1:-1], tp[:, 2:]], axis=2)

    kb, vb = band(k), band(v)
    rel = jnp.arange(3 * BLOCK)[None, :] - BLOCK - jnp.arange(BLOCK)[:, None]
    key_pos = jnp.arange(nb)[:, None] * BLOCK - BLOCK + jnp.arange(3 * BLOCK)[None, :]
    mask = (jnp.abs(rel) <= WINDOW)[None, :, :] & ((key_pos >= 0) & (key_pos < s))[:, None, :]
    s_loc = jnp.einsum('bnqhgd,bnkhd->bnhgqk', qb, kb).astype(jnp.float32) * SWA_SCALE
    s_loc = jnp.where(mask[None, :, None, None], s_loc, NEG_INF)
    s_ctx = jnp.einsum('bnqhgd,bkhd->bnhgqk', qb, k_ctx).astype(jnp.float32) * SWA_SCALE
    sink_l = jnp.broadcast_to(sink.astype(jnp.float32).reshape(1, 1, hkv, g, 1, 1), s_loc.shape[:-1] + (1,))
    p = jax.nn.softmax(jnp.concatenate([s_loc, s_ctx, sink_l], axis=-1), axis=-1).astype(v.dtype)
    p_loc = p[..., :3 * BLOCK]
    p_ctx = p[..., 3 * BLOCK:3 * BLOCK + c_len]
    out = (jnp.einsum('bnhgqk,bnkhd->bnqhgd', p_loc, vb)
           + jnp.einsum('bnhgqk,bkhd->bnqhgd', p_ctx, v_ctx))
    return out.reshape(b, s, hq * d)


def swa_context(q, k, v, sink):
    b, c_len, hq, d = q.shape
    hkv = k.shape[2]
    g = hq // hkv
    qg = q.reshape(b, c_len, hkv, g, d)
    s = jnp.einsum('bqhgd,bkhd->bhgqk', qg, k).astype(jnp.float32) * SWA_SCALE
    sink_l = jnp.broadcast_to(sink.astype(jnp.float32).reshape(1, hkv, g, 1, 1), s.shape[:-1] + (1,))
    p = jax.nn.softmax(jnp.concatenate([s, sink_l], axis=-1), axis=-1)[..., :-1].astype(v.dtype)
    return jnp.einsum('bhgqk,bkhd->bqhgd', p, v).reshape(b, c_len, hq * d)


def merge_branches(o_mla, o_swa, gate_logits, b_gate, w_o_mla, w_o_swa, w_out):
    gates = jax.nn.sigmoid(gate_logits + b_gate)
    g_a, g_b = jnp.split(gates, 2, axis=-1)
    return (g_a * (o_mla @ w_o_mla) + g_b * (o_swa @ w_o_swa)) @ w_out


def dwconv(u, w, bias):
    L = u.shape[1]
    pad = CONV_W // 2
    up = jnp.pad(u, ((0, 0), (pad, pad), (0, 0)))
    out = bias
    for j in range(CONV_W):
        out = out + up[:, j:j + L] * w[j]
    return out


def conv_ffn(h, w_up, conv_w, conv_b, w_down):
    u = dwconv(h @ w_up, conv_w, conv_b)
    gate, val = jnp.split(u, 2, axis=-1)
    return (jax.nn.silu(gate) * val) @ w_down


def setup_inputs(seed: int = 0) -> dict:
    key = jax.random.key(seed)
    ks = jax.random.split(key, 24)
    f32 = jnp.float32

    def w(k, shape, fan_in):
        return jax.random.normal(k, shape, f32) * fan_in ** -0.5

    def gain(k, n):
        return 1.0 + 0.1 * jax.random.normal(k, (DEPTH, n), f32)

    return {
        'x': jax.random.normal(ks[0], (BATCH, SEQ, D_MODEL), f32),
        'c': jax.random.normal(ks[1], (BATCH, D_MODEL), f32),
        'ctx': jax.random.normal(ks[2], (BATCH, CTX_LEN, D_MODEL), f32),
        'c_ctx': jax.random.normal(ks[3], (D_MODEL,), f32),
        'w_ada': w(ks[4], (DEPTH, D_MODEL, 6 * D_MODEL), D_MODEL),
        'b_ada': 0.02 * jax.random.normal(ks[5], (DEPTH, 6 * D_MODEL), f32),
        'attn_pre_g': gain(ks[6], D_MODEL),
        'attn_post_g': gain(ks[7], D_MODEL),
        'w_in': w(ks[8], (DEPTH, D_MODEL, D_IN), D_MODEL),
        'b_gate': 0.02 * jax.random.normal(ks[9], (DEPTH, W_GATE), f32),
        'mla_q_norm_g': gain(ks[10], MLA_Q_RANK),
        'mla_w_q_up': w(ks[11], (DEPTH, MLA_Q_RANK, MLA_HEADS * (MLA_NOPE + MLA_ROPE)), MLA_Q_RANK),
        'mla_kv_norm_g': gain(ks[12], MLA_KV_RANK),
        'mla_w_kv_up': w(ks[13], (DEPTH, MLA_KV_RANK, MLA_HEADS * (MLA_NOPE + MLA_V)), MLA_KV_RANK),
        'mla_w_o': w(ks[14], (DEPTH, MLA_HEADS * MLA_V, D_MODEL), MLA_HEADS * MLA_V),
        'swa_sink': 0.5 * jax.random.normal(ks[15], (DEPTH, SWA_HEADS), f32),
        'swa_w_o': w(ks[16], (DEPTH, SWA_HEADS * SWA_HEAD_DIM, D_MODEL), SWA_HEADS * SWA_HEAD_DIM),
        'w_out': w(ks[17], (DEPTH, D_MODEL, D_MODEL), D_MODEL),
        'ffn_pre_g': gain(ks[18], D_MODEL),
        'ffn_post_g': gain(ks[19], D_MODEL),
        'ffn_w_up': w(ks[20], (DEPTH, D_MODEL, 2 * D_FF), D_MODEL),
        'ffn_conv_w': w(ks[21], (DEPTH, CONV_W, 2 * D_FF), CONV_W),
        'ffn_conv_b': 0.02 * jax.random.normal(ks[22], (DEPTH, 2 * D_FF), f32),
        'ffn_w_down': w(ks[23], (DEPTH, D_FF, D_MODEL), D_FF),
    }


def reference(x, c, ctx, c_ctx, w_ada, b_ada, attn_pre_g, attn_post_g, w_in, b_gate,
              mla_q_norm_g, mla_w_q_up, mla_kv_norm_g, mla_w_kv_up, mla_w_o, swa_sink, swa_w_o,
              w_out, ffn_pre_g, ffn_post_g, ffn_w_up, ffn_conv_w, ffn_conv_b, ffn_w_down):
    s_len = x.shape[1]
    rope_mla = rope_tables(s_len, MLA_ROPE, x.dtype)
    rope_swa = rope_tables(s_len, SWA_HEAD_DIM, x.dtype)
    xc = ctx
    for l in range(DEPTH):
        last = l == DEPTH - 1
        mod = jax.nn.silu(c) @ w_ada[l] + b_ada[l]
        mod_c = (jax.nn.silu(c_ctx) @ w_ada[l] + b_ada[l])[None]
        sh1, sc1, g1, sh2, sc2, g2 = jnp.split(mod, 6, axis=-1)
        csh1, csc1, cg1, csh2, csc2, cg2 = jnp.split(mod_c, 6, axis=-1)

        h_lat = modulate(rmsnorm(x, attn_pre_g[l]), sh1, sc1)
        h_ctx = modulate(rmsnorm(xc, attn_pre_g[l]), csh1, csc1)
        p_lat = jnp.split(h_lat @ w_in[l], SPLITS, axis=-1)
        p_ctx = jnp.split(h_ctx @ w_in[l], SPLITS, axis=-1)

        q_ml, k_ml, v_ml = mla_heads(p_lat[0], p_lat[1], p_lat[2], mla_q_norm_g[l], mla_w_q_up[l],
                                     mla_kv_norm_g[l], mla_w_kv_up[l], rope_mla)
        q_mc, k_mc, v_mc = mla_heads(p_ctx[0], p_ctx[1], p_ctx[2], mla_q_norm_g[l], mla_w_q_up[l],
                                     mla_kv_norm_g[l], mla_w_kv_up[l], None)
        q_sl, k_sl, v_sl = swa_heads(p_lat[3], p_lat[4], p_lat[5], rope_swa)
        q_sc, k_sc, v_sc = swa_heads(p_ctx[3], p_ctx[4], p_ctx[5], None)

        o_mla_l = mla_latent(q_ml, jnp.concatenate([k_mc, k_ml], axis=1), jnp.concatenate([v_mc, v_ml], axis=1))
        o_swa_l = swa_latent(q_sl, k_sl, v_sl, k_sc, v_sc, swa_sink[l])
        attn_l = merge_branches(o_mla_l, o_swa_l, p_lat[6], b_gate[l], mla_w_o[l], swa_w_o[l], w_out[l])
        x = x + g1[:, None, :] * rmsnorm(attn_l, attn_post_g[l])

        if not last:
            b, c_len = xc.shape[0], xc.shape[1]
            o_mla_c = dense_attention(q_mc, k_mc, v_mc, MLA_SCALE).reshape(b, c_len, MLA_HEADS * MLA_V)
            o_swa_c = swa_context(q_sc, k_sc, v_sc, swa_sink[l])
            attn_c = merge_branches(o_mla_c, o_swa_c, p_ctx[6], b_gate[l], mla_w_o[l], swa_w_o[l], w_out[l])
            xc = xc + cg1[:, None, :] * rmsnorm(attn_c, attn_post_g[l])
            hf_c = modulate(rmsnorm(xc, ffn_pre_g[l]), csh2, csc2)
            f_c = conv_ffn(hf_c, ffn_w_up[l], ffn_conv_w[l], ffn_conv_b[l], ffn_w_down[l])
            xc = xc + cg2[:, None, :] * rmsnorm(f_c, ffn_post_g[l])

        hf = modulate(rmsnorm(x, ffn_pre_g[l]), sh2, sc2)
        f = conv_ffn(hf, ffn_w_up[l], ffn_conv_w[l], ffn_conv_b[l], ffn_w_down[l])
        x = x + g2[:, None, :] * rmsnorm(f, ffn_post_g[l])
    return x
```

```python
import contextlib
import numpy as np
import concourse.bass as bass
import concourse.mybir as mybir
from concourse.bass_utils import run_bass_kernel_spmd

F32 = mybir.dt.float32
BF16 = mybir.dt.bfloat16
AF = mybir.ActivationFunctionType
ALU = mybir.AluOpType

D = 1024
SEQ = 4096
CTX = 256
NKEY = SEQ + CTX
NKT = NKEY // 128
DFF = 2816
NCC = DFF // 128
EPS = 1e-6
MLA_SCALE = 96 ** -0.5
SWA_SCALE = 64 ** -0.5
TB = 256
NTB = TB // 128
NB2 = SEQ // TB
QB = 512
WSLOT = 4096

C_ADA = 0
C_W1 = 12
C_W1B = 13
C_WQ = 14
C_WQP = 15
C_G = 16
C_QUP = 20
C_KV = 21
C_WOAB = 22
C_WOUT = 24
C_WUP = 26
C_WDN = 37
NCHUNK = 43
DEBUG = False
import os
FSAFE = bool(int(os.environ.get('FSAFE', '1')))
LAST_PROG = None


class Prog:
    ENG = ("pe", "act", "dve", "pool", "sp")

    def __init__(self, nc, es):
        self.nc = nc
        self.es = es
        self.ops = []
        self.last_w = {}
        self.readers = {}
        self.dma_cnt = {}
        self.dma_sem = {}
        self.pending = {}
        self.since_barrier_dma = {}
        self.tag = ''

    def _rec(self, eng, fn, r, w, dma_key=None):
        idx = len(self.ops)
        deps = set()
        raw = set()
        for k in r:
            lw = self.last_w.get(k)
            if lw is not None:
                deps.add(lw)
                raw.add(lw)
        for k in w:
            lw = self.last_w.get(k)
            if lw is not None:
                deps.add(lw)
            for rd in self.readers.get(k, {}).values():
                deps.add(rd)
        pb = self.pending.pop(eng, None)
        if pb:
            deps |= pb
            raw |= pb
        rk = (eng, dma_key) if dma_key is not None else eng
        for k in r:
            self.readers.setdefault(k, {})[rk] = idx
        for k in w:
            self.last_w[k] = idx
            self.readers[k] = {}
        sig = None
        if dma_key is not None:
            if dma_key not in self.dma_sem:
                self.dma_sem[dma_key] = self.es.enter_context(self.nc.semaphore("d_" + str(len(self.dma_sem))))
            self.dma_cnt[dma_key] = self.dma_cnt.get(dma_key, 0) + 16
            sig = self.dma_cnt[dma_key]
            self.since_barrier_dma[dma_key] = idx
        deps.discard(idx)
        self.ops.append(dict(eng=eng, fn=fn, deps=deps, raw=raw, dma=dma_key, sig=sig, tag=self.tag))
        return idx

    def op(self, eng, fn, r=(), w=()):
        return self._rec(eng, fn, tuple(r), tuple(w))

    def dma(self, eng, out, in_, r, w, key):
        return self._rec(eng, lambda e: e.dma_start(out=out, in_=in_), tuple(r), tuple(w), dma_key=key)

    def barrier(self):
        last = {}
        for i, o in enumerate(self.ops):
            if o["dma"] is None:
                last[o["eng"]] = i
        deps = set(last.values()) | set(self.since_barrier_dma.values())
        self.since_barrier_dma = {}
        for e in self.ENG:
            self.pending[e] = set(deps) | self.pending.get(e, set())

    def emit(self):
        nc = self.nc
        dependents = set()
        for o in self.ops:
            dependents |= o["deps"]
        cnt = {e: 0 for e in self.ENG}
        for i, o in enumerate(self.ops):
            if o["dma"] is None:
                if i in dependents:
                    cnt[o["eng"]] += 1
                    o["sig"] = cnt[o["eng"]]
        sems = {e: self.es.enter_context(nc.semaphore("s_" + e)) for e in self.ENG}
        per = {e: [] for e in self.ENG}
        for i, o in enumerate(self.ops):
            per[o["eng"]].append(i)
        ops = self.ops
        dma_sem = self.dma_sem
        final_dma = dict(self.dma_cnt)

        def run(engname, e):
            waited = {}
            for i in per[engname]:
                o = ops[i]
                for d in sorted(o["deps"]):
                    do = ops[d]
                    if do["dma"] is not None:
                        sem, val, key = dma_sem[do["dma"]], do["sig"], ("d", do["dma"])
                    else:
                        if do["eng"] == engname:
                            if engname == "pe" or d not in o["raw"]:
                                continue
                        sem, val, key = sems[do["eng"]], do["sig"], ("e", do["eng"])
                    if waited.get(key, 0) >= val:
                        continue
                    e.wait_ge(sem, val)
                    waited[key] = val
                ins = o["fn"](e)
                if o["dma"] is not None:
                    ins.then_inc(dma_sem[o["dma"]], 16)
                elif i in dependents:
                    ins.then_inc(sems[engname], 1)
            if engname == "sp":
                for k, v in final_dma.items():
                    if waited.get(("d", k), 0) < v:
                        e.wait_ge(dma_sem[k], v)
                for en in self.ENG:
                    if en != "sp" and cnt[en] > 0:
                        e.wait_ge(sems[en], cnt[en])

        with nc.Block() as block:
            @block.tensor
            def _(e):
                run("pe", e)

            @block.scalar
            def _(e):
                run("act", e)

            @block.vector
            def _(e):
                run("dve", e)

            @block.gpsimd
            def _(e):
                run("pool", e)

            @block.sync
            def _(e):
                run("sp", e)


def build_program():
    nc = bass.Bass("TRN2", target_bir_lowering=False)
    es = contextlib.ExitStack()
    P = Prog(nc, es)
    global LAST_PROG
    LAST_PROG = P

    def dram(name, shape, dt=F32, kind="ExternalInput"):
        return nc.dram_tensor(name, list(shape), dt, kind=kind).ap()

    x_d = dram("x", [SEQ, D])
    ctx_d = dram("ctx", [CTX, D])
    cvec_d = dram("cvec", [128, 16])
    wts_d = dram("wts", [NCHUNK, 128, WSLOT])
    par_d = dram("par", [128, 320])
    tabm_d = dram("tabm", [SEQ // 512, 128, 2, 512])
    tabs_d = dram("tabs", [SEQ // 512, 128, 2, 512])
    cst_d = dram("cst", [128, 320])
    msk_d = dram("msk", [128, 2, 512])
    out_d = dram("out", [SEQ, D], kind="ExternalOutput")
    wbf_d = nc.dram_tensor("wbf", [NCHUNK, 128, WSLOT], BF16, kind="Internal").ap()

    def sb(name, shape, dt):
        return es.enter_context(nc.sbuf_tensor(name, list(shape), dt))

    R = sb("R", [128, 52496], BF16)
    KT = R[:, 0:34816].rearrange("p (h n) -> p h n", h=8)
    Vm = R[:, 34816:52496].rearrange("p (t h d) -> p t h d", t=NKT, h=8)
    o = 0
    x1r = R[:, o:o + 2 * NTB * 2048].bitcast(F32).rearrange("p (s t f) -> p s t f", s=2, t=NTB); o += 2 * NTB * 2048
    aT = R[:, o:o + NCC * TB].rearrange("p (c n) -> p c n", c=NCC); o += NCC * TB
    HFW = 2 * TB + 1
    hfT = R[:, o:o + 8 * HFW].rearrange("p (k n) -> p k n", k=8); o += 8 * HFW
    mergedT = R[:, o:o + 8 * TB].rearrange("p (k n) -> p k n", k=8); o += 8 * TB
    xs = R[:, o:o + 4096].bitcast(F32).rearrange("p (t f) -> p t f", t=2); o += 4096
    qsT = R[:, o:o + 4 * TB].rearrange("p (j n) -> p j n", j=4); o += 4 * TB
    osT = R[:, o:o + 4 * TB].rearrange("p (j n) -> p j n", j=4); o += 4 * TB
    gab = R[:, o:o + 4 * TB].bitcast(F32).rearrange("p (a n) -> p a n", a=2); o += 4 * TB
    wslots = []
    for i in range(6):
        wslots.append(R[:, o:o + WSLOT]); o += WSLOT
    assert o <= 52496, o
    E = sb("E", [128, 6144], BF16)
    wslots = [E[:, 0:WSLOT]] + wslots
    NWS = len(wslots)
    eslot = {0: E[:, 0:WSLOT], 1: E[:, 3584:4608]}

    omT = sb("omT", [128, 4, SEQ], BF16)
    xs1 = omT[:, 0:2, :].rearrange("p a n -> p (a n)").bitcast(F32).rearrange("p (t f) -> p t f", t=4)
    sq = omT[:, 2, 0:2048].bitcast(F32).rearrange("p (a n) -> p a n", a=2)
    rs = omT[:, 2, 2048:3072].bitcast(F32)
    rstd = omT[:, 2, 3072:4096].bitcast(F32)
    hT5 = omT[:, 3, :].rearrange("p (k n) -> p k n", k=8)
    Q = sb("Q", [128, 12288], BF16)
    qan = Q[:, 0:8192].rearrange("p (k n) -> p k n", k=2)
    qT = Q[:, 8192:12288].rearrange("p (h n) -> p h n", h=8)
    KsT = Q[:, 0:4352]
    Vs = Q[:, 4352:8772].rearrange("p (t g d) -> p t g d", t=NKT, g=2)
    Ue = Q[:, 8772:9804].bitcast(F32).rearrange("p (a n) -> p a n", a=2)
    tcv = Q[:, 9804:10828].bitcast(F32).rearrange("p (a n) -> p a n", a=2)
    sg = Q[:, 10828:11340].bitcast(F32)
    hTt = sb("hT", [128, 8, 256], BF16)
    hT = hTt[:]
    hflat = hTt[:].rearrange("p k n -> p (k n)").bitcast(F32)
    tab2a = hflat.rearrange("p (a n) -> p a n", a=2)
    dsc = hflat.rearrange("p (k n) -> p k n", k=8)
    xnt = sb("xn", [128, 1024], BF16)
    xn = xnt[:]
    Pr = sb("Pr", [128, 3, 512], BF16)
    oext = sb("oext", [128, 512], F32)
    rden = sb("rden", [128, 512], F32)
    otmp = sb("otmp", [128, 512], BF16)
    kvn = otmp[:]
    tab = sb("tab", [128, 2, 256], F32)
    t12 = sb("t12", [128, 2, 256], F32)
    tmpot = sb("tmpo", [128, 1024], F32)
    tmpo = tmpot[:]
    t12b = tmpo.rearrange("p (a n) -> p a n", a=2)
    carry = sb("carry", [128, 2 * NCC, 2], F32)
    par = sb("par_sb", [128, 320], F32)
    cst = sb("cst_sb", [128, 320], F32)
    identb = sb("identb", [128, 128], BF16)
    mskb = sb("mskb", [128, 2, 512], BF16)
    cv = sb("cv", [128, 16], F32)
    scb = sb("scb", [128, 8, 2], BF16)
    modv = sb("modv", [128, 2, 48], F32)
    vec = sb("vec", [128, 64], F32)
    grow = sb("grow", [128, 2, 1024], F32)
    small = sb("small", [128, 16], F32)

    ps = es.enter_context(nc.psum_tensor("ps", [128, 8, 512], F32))

    def bank(i):
        return ps[:, i, :]

    def bkey(i):
        return "ps%d" % i

    PAR_PREG, PAR_FPREG, PAR_QG, PAR_KVG, PAR_BGA, PAR_BGB, PAR_CW, PAR_CB, PAR_BADA, PAR_POSTG, PAR_FPOSTG, PAR_SINK = \
        0, 8, 16, 18, 19, 27, 35, 167, 211, 259, 267, 275
    ident_f = cst[:, 0:128]
    ones_f = cst[:, 128:256]
    sel_f = cst[:, 256:320]
    V_GS1, V_SH1, V_CGS1, V_CSH1, V_GS2, V_SH2, V_G1PG, V_G2PG = 0, 8, 16, 24, 32, 40, 48, 56

    def dump(name, ap, rkeys):
        if not DEBUG:
            return
        d_ = nc.dram_tensor("dbg_" + name, list(ap.shape), ap.dtype, kind="ExternalOutput").ap()
        P.dma("sp", d_, ap, tuple(rkeys), (), "dbg_" + name)

    def mm(out, lhsT, rhs, start, stop, r, w):
        P.op("pe", lambda e: e.matmul(out, lhsT, rhs, start=start, stop=stop), r, w)

    def tr(out, in_, r, w):
        P.op("pe", lambda e: e.transpose(out, in_, identb[:]), r, w)

    def act(out, in_, func, r, w, **kw):
        P.op("act", lambda e: e.activation(out=out, in_=in_, func=func, **kw), r, w)

    def tt(out, in0, in1, op, r, w, eng="dve"):
        P.op(eng, lambda e: e.tensor_tensor(out=out, in0=in0, in1=in1, op=op), r, w)

    def ts(out, in0, s1, s2, op0, op1, r, w, eng="dve"):
        if s2 is None:
            P.op(eng, lambda e: e.tensor_scalar(out=out, in0=in0, scalar1=s1, scalar2=None, op0=op0), r, w)
        else:
            P.op(eng, lambda e: e.tensor_scalar(out=out, in0=in0, scalar1=s1, scalar2=s2, op0=op0, op1=op1), r, w)

    def stt(out, in0, scalar, in1, op0, op1, r, w, eng="dve"):
        P.op(eng, lambda e: e.scalar_tensor_tensor(out=out, in0=in0, scalar=scalar, in1=in1, op0=op0, op1=op1), r, w)

    def cp(out, in_, r, w, eng="dve"):
        P.op(eng, lambda e: e.tensor_copy(out=out, in_=in_), r, w)

    def recip(out, in_, r, w):
        P.op("dve", lambda e: e.reciprocal(out=out, in_=in_), r, w)

    def memset(ap, val, w, eng="dve"):
        P.op(eng, lambda e: e.memset(ap, val), (), w)

    bank_rr = [0]

    def nb(lo=0, hi=8):
        b = lo + bank_rr[0] % (hi - lo)
        bank_rr[0] += 1
        return b

    wstate = dict(next=0)

    def wload(chunk, ncols, slot=0):
        P.dma("pool", eslot[slot][:, 0:ncols], wts_d[chunk, :, 0:ncols], (), ("w%d" % slot,), "we%d" % slot)
        return slot

    def wview(slot, pat=None, **kw):
        v = eslot[slot] if isinstance(slot, int) and slot < 2 and pat is None else wslots[slot]
        if pat is None:
            return v
        return v.rearrange(pat, **kw)

    P.tag = 'P0'
    P.dma("sp", par[:], par_d[:], (), ("par",), "c0")
    P.dma("sp", cst[:], cst_d[:], (), ("cst",), "c1")
    P.dma("sp", cv[:], cvec_d[:], (), ("cv",), "c2")
    P.dma("sp", t12b, msk_d[:], (), ("tmpo",), "c3")
    cp(identb[:], ident_f, ("cst",), ("identb",))
    cp(mskb[:], t12b, ("tmpo",), ("mskb",))
    memset(R[:, 34816:52496], 1.0, ("Vm",), eng="pool")
    memset(carry[:], 0.0, ("carry",))
    act(scb[:].rearrange("p k c -> p (k c)"), cv[:], AF.Silu, ("cv",), ("scb",))
    mb = 7
    for j in range(12):
        sl = wload(C_ADA + j, 4096, slot=0)
        wv = eslot[0].rearrange("p (k n) -> p k n", k=8)
        for q in range(4):
            ch = j * 4 + q
            for k in range(8):
                mm(bank(mb)[:, ch * 2:ch * 2 + 2], wv[:, k, q * 128:(q + 1) * 128], scb[:, k, :], k == 0, k == 7,
                   ("w%d" % sl, "scb"), (bkey(mb),))
    mview = bank(mb)[:, 0:96].rearrange("p (c t) -> p c t", t=2)
    tt(modv[:, 0, :], mview[:, :, 0], par[:, PAR_BADA:PAR_BADA + 48], ALU.add, ("par",), (bkey(mb), "modv"))
    tt(modv[:, 1, :], mview[:, :, 1], par[:, PAR_BADA:PAR_BADA + 48], ALU.add, ("par",), (bkey(mb), "modv"))
    def gsv(dst, gcol, scsrc):
        stt(vec[:, dst:dst + 8], scsrc, 1.0, par[:, gcol:gcol + 8], ALU.add, ALU.mult, ("modv", "par"), ("vec",))
    gsv(V_GS1, PAR_PREG, modv[:, 0, 8:16])
    cp(vec[:, V_SH1:V_SH1 + 8], modv[:, 0, 0:8], ("modv",), ("vec",))
    gsv(V_CGS1, PAR_PREG, modv[:, 1, 8:16])
    cp(vec[:, V_CSH1:V_CSH1 + 8], modv[:, 1, 0:8], ("modv",), ("vec",))
    gsv(V_GS2, PAR_FPREG, modv[:, 0, 32:40])
    cp(vec[:, V_SH2:V_SH2 + 8], modv[:, 0, 24:32], ("modv",), ("vec",))
    tt(vec[:, V_G1PG:V_G1PG + 8], modv[:, 0, 16:24], par[:, PAR_POSTG:PAR_POSTG + 8], ALU.mult, ("modv", "par"), ("vec",))
    tt(vec[:, V_G2PG:V_G2PG + 8], modv[:, 0, 40:48], par[:, PAR_FPOSTG:PAR_FPOSTG + 8], ALU.mult, ("modv", "par"), ("vec",))
    for gi, vcol in enumerate((V_G1PG, V_G2PG)):
        for k in range(8):
            ts(dsc[:, k, :], ident_f, vec[:, vcol + k:vcol + k + 1], None, ALU.mult, None, ("cst", "vec"), ("dsc",))
        for k in range(8):
            b_ = 5 + (k // 4)
            mm(bank(b_)[:, (k % 4) * 128:(k % 4 + 1) * 128], ones_f, dsc[:, k, :], True, True, ("cst", "dsc"), (bkey(b_),))
        cp(grow[:, gi, 0:512], bank(5), (), (bkey(5), "grow"))
        cp(grow[:, gi, 512:1024], bank(6), (), (bkey(6), "grow"))
    act(small[64:65, 0:8], par[64:65, PAR_SINK:PAR_SINK + 8], AF.Exp, ("par",), ("small",))

    def prenorm(xsrc, nt, gs_col, sh_col, dst, dst_off, rkeys, wkeys):
        tb = [nb(0, 4) for _ in range(4)]
        for t in range(nt):
            s = 0
            act(xn, xsrc[:, t, :], AF.Square, rkeys, ("xn0", "ss%d" % s), accum_out=small[:, 8 + s:9 + s])
            act(small[:, 10 + s:11 + s], small[:, 8 + s:9 + s], AF.Sqrt, ("ss%d" % s,), ("rs%d" % s,), scale=1.0 / D, bias=EPS)
            recip(small[:, 12 + s:13 + s], small[:, 10 + s:11 + s], ("rs%d" % s,), ("rstd%d" % s,))
            ts(xn, xsrc[:, t, :], small[:, 12 + s:13 + s], None, ALU.mult, None,
               tuple(rkeys) + ("rstd%d" % s,), ("xn%d" % s,))
            for k in range(8):
                b_ = tb[k // 2]
                pv = bank(b_).bitcast(BF16).rearrange("p (a t n) -> p a t n", a=2, t=4)
                tr(pv[:, k % 2, t, :], xn[:, k * 128:(k + 1) * 128], ("xn%d" % s, "identb"), (bkey(b_),))
        for k in range(8):
            b_ = tb[k // 2]
            pv = bank(b_).bitcast(BF16).rearrange("p (a n) -> p a n", a=2)
            src = pv[:, k % 2, 0:nt * 128]
            if k % 2 == 0:
                act(dst[:, k, dst_off:dst_off + nt * 128], src, AF.Identity, ("vec",), (bkey(b_),) + tuple(wkeys),
                    scale=vec[:, gs_col + k:gs_col + k + 1], bias=vec[:, sh_col + k:sh_col + k + 1])
            else:
                ts(dst[:, k, dst_off:dst_off + nt * 128], src, vec[:, gs_col + k:gs_col + k + 1],
                   vec[:, sh_col + k:sh_col + k + 1], ALU.mult, ALU.add, ("vec",), (bkey(b_),) + tuple(wkeys))

    def rms_fm(srcs_banks, nfeat_chunks, n, gcol, dst_fn):
        for c, b_ in enumerate(srcs_banks):
            act(sq[:, c, 0:n], bank(b_)[:, 0:n], AF.Square, (), (bkey(b_), "sq"))
        sb_ = nb(4, 8)
        for c in range(nfeat_chunks):
            mm(bank(sb_)[:, 0:n], ones_f, sq[:, c, 0:n], c == 0, c == nfeat_chunks - 1, ("cst", "sq"), (bkey(sb_),))
        act(rs[:, 0:n], bank(sb_)[:, 0:n], AF.Sqrt, (), (bkey(sb_), "rs"), scale=1.0 / (128 * nfeat_chunks), bias=EPS)
        recip(rstd[:, 0:n], rs[:, 0:n], ("rs",), ("rstd",))
        for c, b_ in enumerate(srcs_banks):
            stt(dst_fn(c), bank(b_)[:, 0:n], par[:, gcol + c:gcol + c + 1], rstd[:, 0:n], ALU.mult, ALU.mult,
                ("par", "rstd"), (bkey(b_), "rmsdst"))

    dump("vec", vec[:], ("vec",))
    dump("grow", grow[:], ("grow",))
    dump("modv", modv[:], ("modv",))
    dump("small", small[:], ("small",))
    P.barrier()
    P.tag = 'P1'
    s_w1 = wload(C_W1, 8 * 448, slot=0)
    s_kv = wload(C_KV, 1024, slot=1)
    for ci_ in list(range(C_WQ, C_QUP)) + list(range(C_WOAB, NCHUNK)):
        P.dma("pool", wbf_d[ci_], wts_d[ci_], (), ("wbf%d" % ci_,), "wbf%d" % ci_)
    w1 = eslot[0][:, 0:8 * 448].rearrange("p (k n) -> p k n", k=8)
    wkv = eslot[1]
    groups1 = [("ctx", 0, 256)] + [("lat", g * 512, 512) for g in range(SEQ // 512)]
    for gi, (kind, t0, n) in enumerate(groups1):
        nt = n // 128
        lat = kind == "lat"
        key0 = (CTX + t0) if lat else 0
        src_d = x_d if lat else ctx_d
        P.dma("sp", xs1[:, 0:nt, :], src_d[t0:t0 + n, :].rearrange("(t p) f -> p t f", p=128), (), ("xs1",), "xs1")
        if lat:
            P.dma("sp", tab2a, tabm_d[t0 // 512], (), ("tab2a",), "tab2a")
        prenorm(xs1, nt, V_GS1 if lat else V_CGS1, V_SH1 if lat else V_CSH1, hT5, 0, ("xs1",), ("hT5",))
        if gi < 2:
            dump("hT_g%d" % gi, hT5, ("hT5",))
        def proj(c0, m):
            b_ = nb(4, 8)
            for k in range(8):
                mm(bank(b_)[0:m, 0:n], w1[:, k, c0:c0 + m], hT5[:, k, 0:n], k == 0, k == 7, ("w%d" % s_w1, "hT5"), (bkey(b_),))
            return b_
        if lat:
            bq = [proj(0, 128), proj(128, 128)]
            rms_fm(bq, 2, n, PAR_QG, lambda c: qan[:, c, t0:t0 + n])
        bkv = proj(256, 128)
        rms_fm([bkv], 1, n, PAR_KVG, lambda c: kvn[:, 0:n])
        ba = proj(320, 96)
        if lat:
            bb = proj(352, 96)
            tt(t12b[64:96, 0, 0:n], bank(ba)[64:96, 0:n], tab2a[64:96, 0, 0:n], ALU.mult, ("tab2a",), (bkey(ba), "t12a"))
            tt(t12b[64:96, 1, 0:n], bank(bb)[64:96, 0:n], tab2a[64:96, 1, 0:n], ALU.mult, ("tab2a",), (bkey(bb), "t12b"))
            for h in range(8):
                tt(KT[64:96, h, key0:key0 + n], t12b[64:96, 0, 0:n], t12b[64:96, 1, 0:n], ALU.add, ("t12a", "t12b"), ("KT",),
                   eng="dve" if h % 2 == 0 else "pool")
        else:
            cp(t12b[64:96, 0, 0:n], bank(ba)[64:96, 0:n], (), (bkey(ba), "t12a"))
            for h in range(8):
                cp(KT[64:96, h, key0:key0 + n], t12b[64:96, 0, 0:n], ("t12a",), ("KT",), eng="dve" if h % 2 == 0 else "pool")
        for h in range(8):
            b_ = nb(4, 8)
            mm(bank(b_)[0:64, 0:n], wkv[:, h * 64:(h + 1) * 64], kvn[:, 0:n], True, True, ("w%d" % s_kv, "rmsdst"), (bkey(b_),))
            if h % 2 == 0:
                act(KT[0:64, h, key0:key0 + n], bank(b_)[0:64, 0:n], AF.Identity, (), (bkey(b_), "KT"))
            else:
                cp(KT[0:64, h, key0:key0 + n], bank(b_)[0:64, 0:n], (), (bkey(b_), "KT"))
        for t in range(nt):
            b_ = nb(4, 8)
            mm(bank(b_), kvn[:, t * 128:(t + 1) * 128], wkv[:, 512:1024], True, True, ("w%d" % s_kv, "rmsdst"), (bkey(b_),))
            cp(Vm[:, key0 // 128 + t, :, 0:64], bank(b_).rearrange("p (h d) -> p h d", h=8), (), (bkey(b_), "Vm"))

    dump("KT", R[:, 0:34816], ("KT",))
    dump("Vm", R[:, 34816:52496], ("Vm",))
    dump("qan", Q[:, 0:8192], ("rmsdst",))
    P.barrier()
    P.tag = 'P2a'
    s_qup = wload(C_QUP, 3072, slot=0)
    wq = eslot[0][:, 0:3072].rearrange("p (k n) -> p k n", k=2)
    SB = (0, 1, 2, 3)
    AB = (4, 5)
    items = [(qb, h, kt) for qb in range(SEQ // QB) for h in range(8) for kt in range(NKT)]
    NI = len(items)

    def qproj(qb, h):
        c0 = qb * QB
        if h == 0:
            P.dma("sp", tab2a, tabm_d[qb], (), ("tab2a",), "tab2a")
        ba, bb = 6, 7
        for k in range(2):
            mm(bank(ba)[0:96, :], wq[:, k, h * 96:(h + 1) * 96], qan[:, k, c0:c0 + QB], k == 0, k == 1,
               ("w%d" % s_qup, "rmsdst"), (bkey(ba),))
        for k in range(2):
            mm(bank(bb)[0:96, :], wq[:, k, 768 + h * 96:768 + (h + 1) * 96], qan[:, k, c0:c0 + QB], k == 0, k == 1,
               ("w%d" % s_qup, "rmsdst"), (bkey(bb),))
        cp(qT[0:64, h, :], bank(ba)[0:64, :], (), (bkey(ba), "qT%d" % h))
        tt(t12b[64:96, 0, :], bank(ba)[64:96, :], tab2a[64:96, 0, :], ALU.mult, ("tab2a",), (bkey(ba), "t12a"))
        tt(t12b[64:96, 1, :], bank(bb)[64:96, :], tab2a[64:96, 1, :], ALU.mult, ("tab2a",), (bkey(bb), "t12b"))
        tt(qT[64:96, h, :], t12b[64:96, 0, :], t12b[64:96, 1, :], ALU.add, ("t12a", "t12b"), ("qT%d" % h,), eng="pool")

    def qk(i):
        qb, h, kt = items[i]
        b_ = SB[i % 4]
        mm(bank(b_), KT[0:96, h, kt * 128:(kt + 1) * 128], qT[0:96, h, :], True, True, ("KT", "qT%d" % h), (bkey(b_),))

    def ex_pv(i):
        qb, h, kt = items[i]
        c0 = qb * QB
        b_ = SB[i % 4]
        pslot = i % 3
        act(Pr[:, pslot, :], bank(b_), AF.Exp, (), (bkey(b_), "P%d" % pslot), scale=MLA_SCALE)
        a_ = AB[h % 2]
        mm(bank(a_)[0:65, :], Vm[:, kt, h, :], Pr[:, pslot, :], kt == 0, kt == NKT - 1, ("Vm", "P%d" % pslot), (bkey(a_),))
        if kt == NKT - 1:
            cp(oext[0:65, :], bank(a_)[0:65, :], (), (bkey(a_), "oext"))
            d_ = 6 + (h % 2)
            mm(bank(d_)[0:64, :], sel_f[0:65, :], oext[0:65, :], True, True, ("cst", "oext"), (bkey(d_),))
            recip(rden[0:64, :], bank(d_)[0:64, :], (), (bkey(d_), "rden"))
            if h % 2 == 0:
                tt(omT[0:64, h // 2, c0:c0 + QB], oext[0:64, :], rden[0:64, :], ALU.mult, ("oext", "rden"), ("omT",))
            else:
                tt(otmp[0:64, :], oext[0:64, :], rden[0:64, :], ALU.mult, ("oext", "rden"), ("otmp",))
                cp(omT[64:128, h // 2, c0:c0 + QB], otmp[0:64, :], ("otmp",), ("omT",), eng="pool")

    LEAD = 2
    qproj(0, 0)
    for i in range(-LEAD, NI):
        if i + LEAD < NI:
            qk(i + LEAD)
        if i >= 0:
            ex_pv(i)
            qb, h, kt = items[i]
            if kt == 6:
                nh = qb * 8 + h + 1
                if nh < (SEQ // QB) * 8:
                    qproj(nh // 8, nh % 8)

    dump("omT", omT[:], ("omT",))
    P.barrier()
    P.tag = 'P1b'
    memset(Q[:, 4352:8772], 1.0, ("Vs",), eng="pool")
    groups = groups1
    xs5 = R[:, 28672:36864].bitcast(F32).rearrange("p (t f) -> p t f", t=4)
    hT5b = R[:, 36864:40960].rearrange("p (k n) -> p k n", k=8)
    s_w1b = wload(C_W1B, 8 * 384, slot=0)
    w1b = eslot[0][:, 0:8 * 384].rearrange("p (k n) -> p k n", k=8)
    for gi, (kind, t0, n) in enumerate(groups):
        nt = n // 128
        lat = kind == "lat"
        key0 = (CTX + t0) if lat else 0
        src_d = x_d if lat else ctx_d
        P.dma("sp", xs5[:, 0:nt, :], src_d[t0:t0 + n, :].rearrange("(t p) f -> p t f", p=128), (), ("xs5",), "xs5")
        if lat:
            P.dma("sp", tab2a, tabs_d[t0 // 512], (), ("tab2a",), "tab2a")
        prenorm(xs5, nt, V_GS1 if lat else V_CGS1, V_SH1 if lat else V_CSH1, hT5b, 0, ("xs5",), ("hT5b",))
        ba = nb(4, 8)
        for k in range(8):
            mm(bank(ba)[:, 0:n], w1b[:, k, 0:128], hT5b[:, k, 0:n], k == 0, k == 7, ("w%d" % s_w1b, "hT5b"), (bkey(ba),))
        if lat:
            bb = nb(4, 8)
            for k in range(8):
                mm(bank(bb)[:, 0:n], w1b[:, k, 128:256], hT5b[:, k, 0:n], k == 0, k == 7, ("w%d" % s_w1b, "hT5b"), (bkey(bb),))
            tt(t12b[:, 0, 0:n], bank(ba)[:, 0:n], tab2a[:, 0, 0:n], ALU.mult, ("tab2a",), (bkey(ba), "t12a"))
            tt(t12b[:, 1, 0:n], bank(bb)[:, 0:n], tab2a[:, 1, 0:n], ALU.mult, ("tab2a",), (bkey(bb), "t12b"))
            tt(KsT[:, key0:key0 + n], t12b[:, 0, 0:n], t12b[:, 1, 0:n], ALU.add, ("t12a", "t12b"), ("KsT",), eng="pool")
        else:
            act(KsT[:, key0:key0 + n], bank(ba)[:, 0:n], AF.Identity, (), (bkey(ba), "KsT"))
        bv = nb(4, 8)
        for t in range(nt):
            for k in range(8):
                mm(bank(bv)[:, t * 128:(t + 1) * 128], hT5b[:, k, t * 128:(t + 1) * 128], w1b[:, k, 256:384], k == 0, k == 7,
                   ("w%d" % s_w1b, "hT5b"), (bkey(bv),))
        cp(Vs[:, key0 // 128:key0 // 128 + nt, :, 0:64],
           bank(bv)[:, 0:nt * 128].rearrange("p (t g d) -> p t g d", t=nt, g=2), (), (bkey(bv), "Vs"))
    P.barrier()

    dump("KsT", KsT, ("KsT",))
    dump("Vs", Q[:, 4352:8772], ("Vs",))
    RING = 5
    sched = []
    ffn_chunks = [(C_WUP + i, 4096) for i in range(11)] + [(C_WDN + i, 4096 if i < 5 else 2048) for i in range(6)]
    back_chunks = [(C_WOAB, 4096), (C_G, 4096), (C_G + 2, 4096), (C_WOAB + 1, 4096), (C_G + 1, 4096), (C_G + 3, 4096),
                   (C_WOUT, 4096), (C_WOUT + 1, 4096)]
    for b_ in range(NB2):
        if b_ >= 2:
            sched += ffn_chunks
        sched += back_chunks
    sched += ffn_chunks + ffn_chunks
    LA = 2
    wst2 = dict(issued=0, used=0)

    def wnext(chunk, ncols=None):
        u = wst2["used"]
        assert sched[u][0] == chunk, (u, sched[u], chunk)
        while wst2["issued"] < min(len(sched), u + 1 + LA):
            i_ = wst2["issued"]
            sl_ = i_ % RING
            P.dma("sp", wslots[sl_][:, 0:sched[i_][1]], wbf_d[sched[i_][0], :, 0:sched[i_][1]], ("wbf%d" % sched[i_][0],), ("w%d" % sl_,), "w%d" % sl_)
            wst2["issued"] += 1
        wst2["used"] += 1
        return u % RING

    P.dma("pool", wslots[5], wbf_d[C_WQ], ("wbf%d" % C_WQ,), ("w5",), "w5")
    P.dma("pool", wslots[6], wbf_d[C_WQP], ("wbf%d" % C_WQP,), ("w6",), "w6")
    wqa = wslots[5].rearrange("p (k n) -> p k n", k=8)
    wqb = wslots[6].rearrange("p (k n) -> p k n", k=8)
    Ue2 = mergedT[:, 0:5, :].rearrange("p k n -> p (k n)").bitcast(F32)[:, 0:2 * (TB + 2)].rearrange("p (a n) -> p a n", a=2)
    frr = [0]

    def nbf():
        frr[0] += 1
        return frr[0] % 4

    def attn_front(b):
        c0 = b * TB
        P.dma("act", xs[:, 0:NTB, :], x_d[c0:c0 + TB, :].rearrange("(t p) f -> p t f", p=128), (), ("xs",), "xs")
        tg, toff = c0 // 512, c0 % 512
        P.dma("act", tab[:, :, 0:TB], tabs_d[tg][:, :, toff:toff + TB], (), ("tab",), "tab")
        prenorm(xs, NTB, V_GS1, V_SH1, hT, 0, ("xs",), ("hT",))
        yield
        for j in range(4):
            ba = nbf()
            for k in range(8):
                mm(bank(ba)[:, 0:TB], wqa[:, k, j * 128:(j + 1) * 128], hT[:, k, 0:TB], k == 0, k == 7, ("w5", "hT"), (bkey(ba),))
            bb = nbf()
            for k in range(8):
                mm(bank(bb)[:, 0:TB], wqb[:, k, j * 128:(j + 1) * 128], hT[:, k, 0:TB], k == 0, k == 7, ("w6", "hT"), (bkey(bb),))
            tt(t12[:, 0, 0:TB], bank(ba)[:, 0:TB], tab[:, 0, 0:TB], ALU.mult, ("tab",), (bkey(ba), "t12a"))
            tt(t12[:, 1, 0:TB], bank(bb)[:, 0:TB], tab[:, 1, 0:TB], ALU.mult, ("tab",), (bkey(bb), "t12b"))
            tt(qsT[:, j, :], t12[:, 0, 0:TB], t12[:, 1, 0:TB], ALU.add, ("t12a", "t12b"), ("qsT",))
            yield
        sw = []
        for g in range(2):
            for qt in range(NTB):
                T = b * NTB + qt
                kts = [(0, 0), (1, 0)] + [(2 + T + rel, rel) for rel in (-1, 0, 1) if 0 <= T + rel < SEQ // 128]
                for ii, (kt, rel) in enumerate(kts):
                    sw.append((g, qt, kt, rel, ii == 0, ii == len(kts) - 1))

        def sqk(i):
            g, qt, kt, rel, first, last = sw[i]
            b_ = i % 2
            mm(bank(b_), KsT[g * 64:(g + 1) * 64, kt * 128:(kt + 1) * 128],
               qsT[g * 64:(g + 1) * 64, :, qt * 128:(qt + 1) * 128], True, True, ("KsT", "qsT"), (bkey(b_),))

        def sexpv(i):
            g, qt, kt, rel, first, last = sw[i]
            b_ = i % 2
            pslot = i % 3
            act(Pr[:, pslot, :], bank(b_), AF.Exp, (), (bkey(b_), "P%d" % pslot), scale=SWA_SCALE)
            if rel != 0 and kt >= 2:
                tt(Pr[:, pslot, :], Pr[:, pslot, :], mskb[:, 0 if rel < 0 else 1, :], ALU.mult, ("mskb",), ("P%d" % pslot,))
            a_ = 2
            mm(bank(a_)[0:65, :], Vs[:, kt, g, :], Pr[:, pslot, :], first, last, ("Vs", "P%d" % pslot), (bkey(a_),))
            if last:
                act(oext[0:65, :], bank(a_)[0:65, :], AF.Identity, (), (bkey(a_), "oext"))
                for j in range(4):
                    ts(oext[64:65, j * 128:(j + 1) * 128], oext[64:65, j * 128:(j + 1) * 128],
                       small[64:65, g * 4 + j:g * 4 + j + 1], None, ALU.add, None, ("small", "oext"), ("oext",))
                d_ = 3
                mm(bank(d_)[0:64, :], sel_f[0:65, :], oext[0:65, :], True, True, ("cst", "oext"), (bkey(d_),))
                recip(rden[0:64, :], bank(d_)[0:64, :], (), (bkey(d_), "rden"))
                if g == 0:
                    tt(osT[0:64, :, qt * 128:(qt + 1) * 128], oext[0:64, :].rearrange("p (j q) -> p j q", j=4),
                       rden[0:64, :].rearrange("p (j q) -> p j q", j=4), ALU.mult, ("oext", "rden"), ("osT",))
                else:
                    tt(otmp[0:64, :], oext[0:64, :], rden[0:64, :], ALU.mult, ("oext", "rden"), ("otmp",))
                    cp(osT[64:128, :, qt * 128:(qt + 1) * 128], otmp[0:64, :].rearrange("p (j q) -> p j q", j=4),
                       ("otmp",), ("osT",), eng="pool")

        for i in range(-1, len(sw)):
            if i + 1 < len(sw):
                sqk(i + 1)
            if i >= 0:
                sexpv(i)
            yield

    def post_residual(b0, xin, xout, gi, rk, wk):
        for hf in range(2):
            act(xn[:, hf * 512:(hf + 1) * 512], bank(b0 + hf), AF.Square, (), (bkey(b0 + hf), "xn0", "ssq%d" % hf),
                accum_out=small[:, 9 + 2 * hf:10 + 2 * hf])
        tt(small[:, 14:15], small[:, 9:10], small[:, 11:12], ALU.add, ("ssq0", "ssq1"), ("ssp",))
        act(small[:, 15:16], small[:, 14:15], AF.Sqrt, ("ssp",), ("rsp",), scale=1.0 / D, bias=EPS)
        recip(small[:, 14:15], small[:, 15:16], ("rsp",), ("ssp",))
        for hf in range(2):
            stt(tmpo[:, hf * 512:(hf + 1) * 512], bank(b0 + hf), small[:, 14:15], grow[:, gi, hf * 512:(hf + 1) * 512],
                ALU.mult, ALU.mult, ("ssp", "grow"), (bkey(b0 + hf), "tmpo"))
        tt(xout, tmpo, xin, ALU.add, ("tmpo",) + tuple(rk), tuple(wk))

    def attn_back(b):
        c0 = b * TB
        xsl = b % 2
        x1 = x1r[:, xsl, :, :]
        xk = "x1_%d" % xsl
        if b == 0:
            dump("qsT", qsT, ("qsT",))
            dump("osT", osT, ("osT",))
        sg_ = {}
        for fo in range(8):
            if fo % 4 == 0:
                s_oab = wnext(C_WOAB + fo // 4, 4096)
                woa = wslots[s_oab][:, 0:2048].rearrange("p (k n) -> p k n", k=4)
                wob = wslots[s_oab][:, 2048:4096].rearrange("p (k n) -> p k n", k=4)
                sg_[0] = wnext(C_G + fo // 4, 4096)
                sg_[1] = wnext(C_G + 2 + fo // 4, 4096)
            wga = wview(sg_[0], "p (k n) -> p k n", k=8)
            wgb = wview(sg_[1], "p (k n) -> p k n", k=8)
            fc = (fo % 4) * 128
            pa, pb_, pga, pgb = nb(), nb(), nb(), nb()
            for k in range(4):
                mm(bank(pa)[:, 0:TB], woa[:, k, fc:fc + 128], omT[:, k, c0:c0 + TB], k == 0, k == 3, ("w%d" % s_oab, "omT"), (bkey(pa),))
            for k in range(4):
                mm(bank(pb_)[:, 0:TB], wob[:, k, fc:fc + 128], osT[:, k, :], k == 0, k == 3, ("w%d" % s_oab, "osT"), (bkey(pb_),))
            for k in range(8):
                mm(bank(pga)[:, 0:TB], wga[:, k, fc:fc + 128], hT[:, k, 0:TB], k == 0, k == 7, ("w%d" % sg_[0], "hT"), (bkey(pga),))
            for k in range(8):
                mm(bank(pgb)[:, 0:TB], wgb[:, k, fc:fc + 128], hT[:, k, 0:TB], k == 0, k == 7, ("w%d" % sg_[1], "hT"), (bkey(pgb),))
            act(gab[:, 0, :], bank(pga)[:, 0:TB], AF.Sigmoid, ("par",), (bkey(pga), "ga"), bias=par[:, PAR_BGA + fo:PAR_BGA + fo + 1])
            act(gab[:, 1, :], bank(pgb)[:, 0:TB], AF.Sigmoid, ("par",), (bkey(pgb), "gb"), bias=par[:, PAR_BGB + fo:PAR_BGB + fo + 1])
            tt(t12[:, 0, 0:TB], bank(pa)[:, 0:TB], gab[:, 0, :], ALU.mult, ("ga",), (bkey(pa), "t12a"))
            tt(t12[:, 1, 0:TB], bank(pb_)[:, 0:TB], gab[:, 1, :], ALU.mult, ("gb",), (bkey(pb_), "t12b"))
            tt(mergedT[:, fo, :], t12[:, 0, 0:TB], t12[:, 1, 0:TB], ALU.add, ("t12a", "t12b"), ("mergedT",))
        s_o0 = wnext(C_WOUT, 4096)
        s_o1 = wnext(C_WOUT + 1, 4096)
        wo = [wview(s_o0, "p (k n) -> p k n", k=8), wview(s_o1, "p (k n) -> p k n", k=8)]
        for t in range(NTB):
            b0 = 2 * (nb() % 4)
            for hf in range(2):
                for k in range(8):
                    mm(bank(b0 + hf), mergedT[:, k, t * 128:(t + 1) * 128], wo[hf][:, k, :], k == 0, k == 7,
                       ("w%d" % (s_o0 if hf == 0 else s_o1), "mergedT"), (bkey(b0 + hf),))
            post_residual(b0, xs[:, t, :], x1[:, t, :], 0, ("xs",), (xk,))
        if b == 0:
            dump("mergedT", mergedT, ("mergedT",))
            dump("x1", x1, (xk,))
        hsl = b % 2
        prenorm(x1, NTB, V_GS2, V_SH2, hfT, hsl * TB, (xk,), ("hf%d" % hsl,))
        if hsl == 0:
            cp(hfT[:, :, 2 * TB:2 * TB + 1], hfT[:, :, 0:1], ("hf0",), ("hfx",), eng="pool")

    def ffn_block(b):
        c0 = b * TB
        xsl = b % 2
        x1 = x1r[:, xsl, :, :]
        xk = "x1_%d" % xsl
        hsl = b % 2
        w0 = hsl * TB + 1
        hkeys = ("hf%d" % hsl, "hf%d" % (1 - hsl)) if hsl == 0 else ("hf1", "hfx")
        if b == NB2 - 1:
            memset(hfT[:, :, 2 * TB:2 * TB + 1], 0.0, ("hfx",), eng="pool")
        ub = [0]
        for i in range(11):
            s_up = wnext(C_WUP + i, 4096)
            wu = wview(s_up, "p (k n) -> p k n", k=8)
            for p_ in range(2):
                c = 2 * i + p_
                ccs = (c, NCC + c)
                bks = []
                for gv in range(2):
                    col = p_ * 256 + gv * 128
                    b_ = 4 + ub[0] % 4
                    ub[0] += 1
                    bks.append(b_)
                    if b == 0:
                        for k in range(8):
                            mm(bank(b_)[:, TB:TB + 1], wu[:, k, col:col + 128], hfT[:, k, 0:1], k == 0, k == 7,
                               ("w%d" % s_up, "hf0"), (bkey(b_),))
                        cp(carry[:, ccs[gv], 1:2], bank(b_)[:, TB:TB + 1], (), (bkey(b_), "carry"))
                    for k in range(8):
                        mm(bank(b_)[:, 0:TB], wu[:, k, col:col + 128], hfT[:, k, w0:w0 + TB], k == 0, k == 7,
                           ("w%d" % s_up,) + hkeys, (bkey(b_),))
                rr = c % 2
                UeR = Ue if rr == 0 else Ue2
                tcR = tcv if rr == 0 else gab
                for gv in range(2):
                    cc = ccs[gv]
                    b_ = bks[gv]
                    uk = ("Ue%d" % gv,) if rr == 0 else ("mergedT",)
                    tk = ("tcv%d" % gv,) if rr == 0 else (("ga",) if gv == 0 else ("gb",))
                    cp(UeR[:, gv, 0:2], carry[:, cc, :], ("carry",), uk)
                    act(UeR[:, gv, 2:TB + 2], bank(b_)[:, 0:TB], AF.Identity, (), (bkey(b_),) + uk)
                    cp(carry[:, cc, :], UeR[:, gv, TB:TB + 2], uk, ("carry",), eng="pool")
                    cw = PAR_CW + cc * 3
                    act(tcR[:, gv, :], UeR[:, gv, 0:TB], AF.Identity, uk + ("par",), tk,
                        scale=par[:, cw:cw + 1], bias=par[:, PAR_CB + cc:PAR_CB + cc + 1])
                    stt(tcR[:, gv, :], UeR[:, gv, 1:TB + 1], par[:, cw + 1:cw + 2], tcR[:, gv, :], ALU.mult, ALU.add,
                        uk + ("par",) + tk, tk)
                    stt(tcR[:, gv, :], UeR[:, gv, 2:TB + 2], par[:, cw + 2:cw + 3], tcR[:, gv, :], ALU.mult, ALU.add,
                        uk + ("par",) + tk, tk)
                tk0 = "tcv0" if rr == 0 else "ga"
                tk1 = "tcv1" if rr == 0 else "gb"
                act(sg, tcR[:, 0, :], AF.Silu, (tk0,), ("sg",))
                tt(aT[:, c, :], sg, tcR[:, 1, :], ALU.mult, ("sg", tk1), ("aT",))
                yield
        if b == 0:
            dump("aT", aT, ("aT",))
            dump("hfT", hfT, ("hf0", "hf1", "hfx"))
        for c in range(NCC):
            if c % 4 == 0:
                s_dn = wnext(C_WDN + c // 4, 4096 if c < 20 else 2048)
                wd = wview(s_dn, "p (k n) -> p k n", k=4)
            for t in range(NTB):
                for hf in range(2):
                    mm(bank(4 + 2 * t + hf), aT[:, c, t * 128:(t + 1) * 128], wd[:, c % 4, hf * 512:(hf + 1) * 512], c == 0, c == NCC - 1,
                       ("w%d" % s_dn, "aT"), (bkey(4 + 2 * t + hf),))
            yield
        for t in range(NTB):
            post_residual(4 + 2 * t, x1[:, t, :], tmpo, 1, (xk,), ("tmpo",))
            P.dma("pool", out_d[c0 + t * 128:c0 + (t + 1) * 128, :], tmpo, ("tmpo",), (), "out")
            yield

    def interleave(ga_, gf_, ratio=1):
        alive_a, alive_f = ga_ is not None, gf_ is not None
        fsteps = 0
        while alive_a or alive_f:
            if alive_a:
                try:
                    next(ga_)
                except StopIteration:
                    alive_a = False
            for _ in range(ratio):
                if alive_f and not (FSAFE and alive_a and fsteps >= 44):
                    try:
                        next(gf_)
                        fsteps += 1
                    except StopIteration:
                        alive_f = False

    for b in range(NB2):
        P.tag = 'A%d' % b
        interleave(attn_front(b), ffn_block(b - 2) if b >= 2 else None, ratio=2)
        P.tag = 'B%d' % b
        attn_back(b)
    P.tag = 'F%d' % (NB2 - 2)
    interleave(None, ffn_block(NB2 - 2))
    P.tag = 'F%d' % (NB2 - 1)
    interleave(None, ffn_block(NB2 - 1))

    P.emit()
    es.close()
    return nc


def _rope_tables(rot_dim):
    rows = SEQ // 64
    row = np.repeat(np.arange(rows, dtype=np.float32), 64)
    col = np.tile(np.arange(64, dtype=np.float32), rows)
    quarter = rot_dim // 4
    inv = (10000.0 ** (-np.arange(quarter, dtype=np.float32) / quarter)).astype(np.float32)
    ar = row[:, None] * inv
    ac = col[:, None] * inv
    ang = np.concatenate([ar, ar, ac, ac], axis=-1)
    cos = np.cos(ang).astype(np.float32)
    sin = np.sin(ang).astype(np.float32)
    q = quarter
    sgn = np.concatenate([-np.ones(q), np.ones(q), -np.ones(q), np.ones(q)]).astype(np.float32)
    return cos, sin * sgn


def _perm_idx(rot_dim):
    q = rot_dim // 4
    a = np.arange(rot_dim)
    return np.concatenate([a[q:2 * q], a[0:q], a[3 * q:4 * q], a[2 * q:3 * q]])


def _chunk_k(w, ncols_pad=None):
    kk = w.shape[0] // 128
    n = w.shape[1]
    return np.ascontiguousarray(w.reshape(kk, 128, n).transpose(1, 0, 2).reshape(128, kk * n))


def _prep_shared(inp):
    f = np.float32
    w_in = inp["w_in"][0]
    wts = np.zeros((NCHUNK, 128, WSLOT), f)

    def put(ci, arr):
        wts[ci, :, :arr.shape[1]] = arr

    w_ada = inp["w_ada"][0]
    for j in range(12):
        put(C_ADA + j, _chunk_k(w_ada[:, j * 512:(j + 1) * 512]))
    s_qa, s_kva, s_kr, s_sq, s_sk, s_sv = 0, 256, 384, 416, 928, 1056
    s_gate = 1184
    pm = _perm_idx(32)
    ps_ = _perm_idx(64)
    kr = w_in[:, s_kr:s_kr + 32]
    W1 = np.concatenate([w_in[:, 0:256], w_in[:, s_kva:s_kva + 128], kr, kr[:, pm]], axis=1)
    put(C_W1, _chunk_k(W1))
    sk = w_in[:, s_sk:s_sk + 128].reshape(D, 2, 64)
    W1B = np.concatenate([sk.reshape(D, 128), sk[:, :, ps_].reshape(D, 128), w_in[:, s_sv:s_sv + 128]], axis=1)
    put(C_W1B, _chunk_k(W1B))
    sqw = w_in[:, s_sq:s_sq + 512].reshape(D, 8, 64)
    order = [0, 4, 1, 5, 2, 6, 3, 7]
    put(C_WQ, _chunk_k(sqw[:, order, :].reshape(D, 512)))
    put(C_WQP, _chunk_k(sqw[:, order, :][:, :, ps_].reshape(D, 512)))
    for i in range(4):
        put(C_G + i, _chunk_k(w_in[:, s_gate + i * 512:s_gate + (i + 1) * 512]))
    qu = inp["mla_w_q_up"][0].reshape(256, 8, 96)
    qub = qu.copy()
    qub[:, :, 64:96] = qu[:, :, 64:96][:, :, pm]
    QUP = np.concatenate([qu.reshape(256, 768), qub.reshape(256, 768)], axis=1)
    put(C_QUP, np.ascontiguousarray(QUP.reshape(2, 128, 1536).transpose(1, 0, 2).reshape(128, 3072)))
    kvu = inp["mla_w_kv_up"][0].reshape(128, 8, 128)
    put(C_KV, np.concatenate([kvu[:, :, 0:64].reshape(128, 512), kvu[:, :, 64:128].reshape(128, 512)], axis=1))
    woa = inp["mla_w_o"][0]
    woa_r = woa.reshape(4, 128, D).transpose(1, 0, 2)
    wob = inp["swa_w_o"][0].reshape(2, 4, 64, D)
    wob_r = wob.transpose(0, 2, 1, 3).reshape(128, 4, D)
    for hh in range(2):
        put(C_WOAB + hh, np.concatenate([woa_r[:, :, hh * 512:(hh + 1) * 512].reshape(128, 2048),
                                         wob_r[:, :, hh * 512:(hh + 1) * 512].reshape(128, 2048)], axis=1))
    wout = inp["w_out"][0]
    put(C_WOUT, _chunk_k(wout[:, 0:512]))
    put(C_WOUT + 1, _chunk_k(wout[:, 512:1024]))
    wup = inp["ffn_w_up"][0]
    for i in range(11):
        cols = []
        for p_ in range(2):
            c = 2 * i + p_
            cols.append(wup[:, c * 128:(c + 1) * 128])
            cols.append(wup[:, DFF + c * 128:DFF + (c + 1) * 128])
        put(C_WUP + i, _chunk_k(np.concatenate(cols, axis=1)))
    wdn = inp["ffn_w_down"][0].reshape(NCC, 128, D)
    for i in range(6):
        cs = wdn[i * 4:(i + 1) * 4]
        put(C_WDN + i, np.ascontiguousarray(cs.transpose(1, 0, 2).reshape(128, cs.shape[0] * D)))

    par = np.zeros((128, 256), f)

    def fm(v, n):
        return np.ascontiguousarray(np.asarray(v, f).reshape(n, 128).T)

    par[:, 0:8] = fm(inp["attn_pre_g"][0], 8)
    par[:, 8:16] = fm(inp["ffn_pre_g"][0], 8)
    par[:, 16:18] = fm(inp["mla_q_norm_g"][0], 2)
    par[:, 18:19] = fm(inp["mla_kv_norm_g"][0], 1)
    par[:, 19:27] = fm(inp["b_gate"][0][0:1024], 8)
    par[:, 27:35] = fm(inp["b_gate"][0][1024:2048], 8)
    cw = inp["ffn_conv_w"][0]
    cwf = cw.reshape(3, 2 * NCC, 128).transpose(2, 1, 0)
    par[:, 35:35 + 132] = cwf.reshape(128, 132)
    par[:, 167:167 + 44] = fm(inp["ffn_conv_b"][0], 44)
    par[:, 211:211 + 48] = 0
    return wts, par


def _prep(inp):
    f = np.float32
    wts, par = _prep_shared(inp)
    return wts, par


def kernel(**inputs):
    f = np.float32
    inp = {k: np.asarray(v) for k, v in inputs.items()}
    wts, par = _prep_shared(inp)
    bada = np.ascontiguousarray(inp["b_ada"][0].astype(f).reshape(48, 128).T)
    post_g = np.ascontiguousarray(inp["attn_post_g"][0].astype(f).reshape(8, 128).T)
    fpost_g = np.ascontiguousarray(inp["ffn_post_g"][0].astype(f).reshape(8, 128).T)
    par2 = np.zeros((128, 320), f)
    par2[:, 0:211] = par[:, 0:211]
    par2[:, 211:259] = bada
    par2[:, 259:267] = post_g
    par2[:, 267:275] = fpost_g
    par2[64, 275:283] = inp["swa_sink"][0].astype(f)
    cst = np.zeros((128, 320), f)
    cst[:, 0:128] = np.eye(128, dtype=f)
    cst[:, 128:256] = 1.0
    cst[64, 256:320] = 1.0
    kk = np.arange(128)[:, None]
    qq = np.arange(128)[None, :]
    msk = np.zeros((128, 2, 512), f)
    msk[:, 0, :] = np.tile((qq <= kk).astype(f), (1, 4))
    msk[:, 1, :] = np.tile((kk <= qq).astype(f), (1, 4))
    cm, sm = _rope_tables(32)
    cs_, ss_ = _rope_tables(64)
    tabm = np.zeros((SEQ // 512, 128, 2, 512), f)
    tabs = np.zeros((SEQ // 512, 128, 2, 512), f)
    for g in range(SEQ // 512):
        sl = slice(g * 512, (g + 1) * 512)
        tabm[g, 64:96, 0, :] = cm[sl].T
        tabm[g, 64:96, 1, :] = sm[sl].T
        tabs[g, 0:64, 0, :] = cs_[sl].T
        tabs[g, 64:128, 0, :] = cs_[sl].T
        tabs[g, 0:64, 1, :] = ss_[sl].T
        tabs[g, 64:128, 1, :] = ss_[sl].T
    nc = build_program()
    in_maps = []
    for b in range(8):
        cvec = np.zeros((128, 16), f)
        cvec.reshape(128, 8, 2)[:, :, 0] = inp["c"][b].astype(f).reshape(8, 128).T
        cvec.reshape(128, 8, 2)[:, :, 1] = inp["c_ctx"].astype(f).reshape(8, 128).T
        in_maps.append({
            "x": np.ascontiguousarray(inp["x"][b], dtype=f),
            "ctx": np.ascontiguousarray(inp["ctx"][b], dtype=f),
            "cvec": cvec, "wts": wts, "par": par2, "tabm": tabm, "tabs": tabs, "cst": cst, "msk": msk,
        })
    res = run_bass_kernel_spmd(nc, in_maps, core_ids=list(range(8)))
    return np.stack([np.asarray(r["out"], dtype=f) for r in res.results], axis=0)
```

```python
import contextlib
import numpy as np
import concourse.bass as bass
import concourse.mybir as mybir
from concourse.bass_utils import run_bass_kernel_spmd

F32 = mybir.dt.float32
BF16 = mybir.dt.bfloat16
AF = mybir.ActivationFunctionType
ALU = mybir.AluOpType

D = 1024
SEQ = 4096
CTX = 256
NKEY = SEQ + CTX
NKT = NKEY // 128
DFF = 2816
NCC = DFF // 128
EPS = 1e-6
MLA_SCALE = 96 ** -0.5
SWA_SCALE = 64 ** -0.5
TB = 256
NTB = TB // 128
NB2 = SEQ // TB
QB = 512
WSLOT = 4096

C_ADA = 0
C_W1 = 12
C_W1B = 13
C_WQ = 14
C_WQP = 15
C_G = 16
C_QUP = 20
C_KV = 21
C_WOAB = 22
C_WOUT = 24
C_WUP = 26
C_WDN = 37
NCHUNK = 43
DEBUG = False
FSAFE = True
LAST_PROG = None


class Prog:
    ENG = ("pe", "act", "dve", "pool", "sp")

    def __init__(self, nc, es):
        self.nc = nc
        self.es = es
        self.ops = []
        self.last_w = {}
        self.readers = {}
        self.dma_cnt = {}
        self.dma_sem = {}
        self.pending = {}
        self.since_barrier_dma = {}
        self.tag = ''

    def _rec(self, eng, fn, r, w, dma_key=None):
        idx = len(self.ops)
        deps = set()
        raw = set()
        for k in r:
            lw = self.last_w.get(k)
            if lw is not None:
                deps.add(lw)
                raw.add(lw)
        for k in w:
            lw = self.last_w.get(k)
            if lw is not None:
                deps.add(lw)
            for rd in self.readers.get(k, {}).values():
                deps.add(rd)
        pb = self.pending.pop(eng, None)
        if pb:
            deps |= pb
            raw |= pb
        rk = (eng, dma_key) if dma_key is not None else eng
        for k in r:
            self.readers.setdefault(k, {})[rk] = idx
        for k in w:
            self.last_w[k] = idx
            self.readers[k] = {}
        sig = None
        if dma_key is not None:
            if dma_key not in self.dma_sem:
                self.dma_sem[dma_key] = self.es.enter_context(self.nc.semaphore("d_" + str(len(self.dma_sem))))
            self.dma_cnt[dma_key] = self.dma_cnt.get(dma_key, 0) + 16
            sig = self.dma_cnt[dma_key]
            self.since_barrier_dma[dma_key] = idx
        deps.discard(idx)
        self.ops.append(dict(eng=eng, fn=fn, deps=deps, raw=raw, dma=dma_key, sig=sig, tag=self.tag))
        return idx

    def op(self, eng, fn, r=(), w=()):
        return self._rec(eng, fn, tuple(r), tuple(w))

    def dma(self, eng, out, in_, r, w, key):
        return self._rec(eng, lambda e: e.dma_start(out=out, in_=in_), tuple(r), tuple(w), dma_key=key)

    def barrier(self):
        last = {}
        for i, o in enumerate(self.ops):
            if o["dma"] is None:
                last[o["eng"]] = i
        deps = set(last.values()) | set(self.since_barrier_dma.values())
        self.since_barrier_dma = {}
        for e in self.ENG:
            self.pending[e] = set(deps) | self.pending.get(e, set())

    def emit(self):
        nc = self.nc
        dependents = set()
        for o in self.ops:
            dependents |= o["deps"]
        cnt = {e: 0 for e in self.ENG}
        for i, o in enumerate(self.ops):
            if o["dma"] is None:
                if i in dependents:
                    cnt[o["eng"]] += 1
                    o["sig"] = cnt[o["eng"]]
        sems = {e: self.es.enter_context(nc.semaphore("s_" + e)) for e in self.ENG}
        per = {e: [] for e in self.ENG}
        for i, o in enumerate(self.ops):
            per[o["eng"]].append(i)
        ops = self.ops
        dma_sem = self.dma_sem
        final_dma = dict(self.dma_cnt)

        def run(engname, e):
            waited = {}
            for i in per[engname]:
                o = ops[i]
                for d in sorted(o["deps"]):
                    do = ops[d]
                    if do["dma"] is not None:
                        sem, val, key = dma_sem[do["dma"]], do["sig"], ("d", do["dma"])
                    else:
                        if do["eng"] == engname:
                            if engname == "pe" or d not in o["raw"]:
                                continue
                        sem, val, key = sems[do["eng"]], do["sig"], ("e", do["eng"])
                    if waited.get(key, 0) >= val:
                        continue
                    e.wait_ge(sem, val)
                    waited[key] = val
                ins = o["fn"](e)
                if o["dma"] is not None:
                    ins.then_inc(dma_sem[o["dma"]], 16)
                elif i in dependents:
                    ins.then_inc(sems[engname], 1)
            if engname == "sp":
                for k, v in final_dma.items():
                    if waited.get(("d", k), 0) < v:
                        e.wait_ge(dma_sem[k], v)
                for en in self.ENG:
                    if en != "sp" and cnt[en] > 0:
                        e.wait_ge(sems[en], cnt[en])

        with nc.Block() as block:
            @block.tensor
            def _(e):
                run("pe", e)

            @block.scalar
            def _(e):
                run("act", e)

            @block.vector
            def _(e):
                run("dve", e)

            @block.gpsimd
            def _(e):
                run("pool", e)

            @block.sync
            def _(e):
                run("sp", e)


def build_program():
    nc = bass.Bass("TRN2", target_bir_lowering=False)
    es = contextlib.ExitStack()
    P = Prog(nc, es)
    global LAST_PROG
    LAST_PROG = P

    def dram(name, shape, dt=F32, kind="ExternalInput"):
        return nc.dram_tensor(name, list(shape), dt, kind=kind).ap()

    x_d = dram("x", [SEQ, D])
    ctx_d = dram("ctx", [CTX, D])
    cvec_d = dram("cvec", [128, 16])
    wts_d = dram("wts", [NCHUNK, 128, WSLOT])
    par_d = dram("par", [128, 320])
    tabm_d = dram("tabm", [SEQ // 512, 128, 2, 512])
    tabs_d = dram("tabs", [SEQ // 512, 128, 2, 512])
    cst_d = dram("cst", [128, 320])
    msk_d = dram("msk", [128, 2, 512])
    out_d = dram("out", [SEQ, D], kind="ExternalOutput")
    wbf_d = nc.dram_tensor("wbf", [NCHUNK, 128, WSLOT], BF16, kind="Internal").ap()

    def sb(name, shape, dt):
        return es.enter_context(nc.sbuf_tensor(name, list(shape), dt))

    R = sb("R", [128, 52496], BF16)
    KT = R[:, 0:34816].rearrange("p (h n) -> p h n", h=8)
    Vm = R[:, 34816:52496].rearrange("p (t h d) -> p t h d", t=NKT, h=8)
    o = 0
    x1r = R[:, o:o + 2 * NTB * 2048].bitcast(F32).rearrange("p (s t f) -> p s t f", s=2, t=NTB); o += 2 * NTB * 2048
    aT = R[:, o:o + NCC * TB].rearrange("p (c n) -> p c n", c=NCC); o += NCC * TB
    HFW = 2 * TB + 1
    hfT = R[:, o:o + 8 * HFW].rearrange("p (k n) -> p k n", k=8); o += 8 * HFW
    mergedT = R[:, o:o + 8 * TB].rearrange("p (k n) -> p k n", k=8); o += 8 * TB
    xs = R[:, o:o + 4096].bitcast(F32).rearrange("p (t f) -> p t f", t=2); o += 4096
    qsT = R[:, o:o + 4 * TB].rearrange("p (j n) -> p j n", j=4); o += 4 * TB
    osT = R[:, o:o + 4 * TB].rearrange("p (j n) -> p j n", j=4); o += 4 * TB
    gab = R[:, o:o + 4 * TB].bitcast(F32).rearrange("p (a n) -> p a n", a=2); o += 4 * TB
    wslots = []
    for i in range(6):
        wslots.append(R[:, o:o + WSLOT]); o += WSLOT
    assert o <= 52496, o
    E = sb("E", [128, 6144], BF16)
    wslots = [E[:, 0:WSLOT]] + wslots
    NWS = len(wslots)
    eslot = {0: E[:, 0:WSLOT], 1: E[:, 3584:4608]}

    omT = sb("omT", [128, 4, SEQ], BF16)
    xs1 = omT[:, 0:2, :].rearrange("p a n -> p (a n)").bitcast(F32).rearrange("p (t f) -> p t f", t=4)
    sq = omT[:, 2, 0:2048].bitcast(F32).rearrange("p (a n) -> p a n", a=2)
    rs = omT[:, 2, 2048:3072].bitcast(F32)
    rstd = omT[:, 2, 3072:4096].bitcast(F32)
    hT5 = omT[:, 3, :].rearrange("p (k n) -> p k n", k=8)
    Q = sb("Q", [128, 12288], BF16)
    qan = Q[:, 0:8192].rearrange("p (k n) -> p k n", k=2)
    qT = Q[:, 8192:12288].rearrange("p (h n) -> p h n", h=8)
    KsT = Q[:, 0:4352]
    Vs = Q[:, 4352:8772].rearrange("p (t g d) -> p t g d", t=NKT, g=2)
    Ue = Q[:, 8772:9804].bitcast(F32).rearrange("p (a n) -> p a n", a=2)
    tcv = Q[:, 9804:10828].bitcast(F32).rearrange("p (a n) -> p a n", a=2)
    sg = Q[:, 10828:11340].bitcast(F32)
    hTt = sb("hT", [128, 8, 256], BF16)
    hT = hTt[:]
    hflat = hTt[:].rearrange("p k n -> p (k n)").bitcast(F32)
    tab2a = hflat.rearrange("p (a n) -> p a n", a=2)
    dsc = hflat.rearrange("p (k n) -> p k n", k=8)
    xnt = sb("xn", [128, 1024], BF16)
    xn = xnt[:]
    Pr = sb("Pr", [128, 3, 512], BF16)
    oext = sb("oext", [128, 512], F32)
    rden = sb("rden", [128, 512], F32)
    otmp = sb("otmp", [128, 512], BF16)
    kvn = otmp[:]
    tab = sb("tab", [128, 2, 256], F32)
    t12 = sb("t12", [128, 2, 256], F32)
    tmpot = sb("tmpo", [128, 1024], F32)
    tmpo = tmpot[:]
    t12b = tmpo.rearrange("p (a n) -> p a n", a=2)
    carry = sb("carry", [128, 2 * NCC, 2], F32)
    par = sb("par_sb", [128, 320], F32)
    cst = sb("cst_sb", [128, 320], F32)
    identb = sb("identb", [128, 128], BF16)
    mskb = sb("mskb", [128, 2, 512], BF16)
    cv = sb("cv", [128, 16], F32)
    scb = sb("scb", [128, 8, 2], BF16)
    modv = sb("modv", [128, 2, 48], F32)
    vec = sb("vec", [128, 64], F32)
    grow = sb("grow", [128, 2, 1024], F32)
    small = sb("small", [128, 16], F32)

    ps = es.enter_context(nc.psum_tensor("ps", [128, 8, 512], F32))

    def bank(i):
        return ps[:, i, :]

    def bkey(i):
        return "ps%d" % i

    PAR_PREG, PAR_FPREG, PAR_QG, PAR_KVG, PAR_BGA, PAR_BGB, PAR_CW, PAR_CB, PAR_BADA, PAR_POSTG, PAR_FPOSTG, PAR_SINK = \
        0, 8, 16, 18, 19, 27, 35, 167, 211, 259, 267, 275
    ident_f = cst[:, 0:128]
    ones_f = cst[:, 128:256]
    sel_f = cst[:, 256:320]
    V_GS1, V_SH1, V_CGS1, V_CSH1, V_GS2, V_SH2, V_G1PG, V_G2PG = 0, 8, 16, 24, 32, 40, 48, 56

    def dump(name, ap, rkeys):
        if not DEBUG:
            return
        d_ = nc.dram_tensor("dbg_" + name, list(ap.shape), ap.dtype, kind="ExternalOutput").ap()
        P.dma("sp", d_, ap, tuple(rkeys), (), "dbg_" + name)

    def mm(out, lhsT, rhs, start, stop, r, w):
        P.op("pe", lambda e: e.matmul(out, lhsT, rhs, start=start, stop=stop), r, w)

    def tr(out, in_, r, w):
        P.op("pe", lambda e: e.transpose(out, in_, identb[:]), r, w)

    def act(out, in_, func, r, w, **kw):
        P.op("act", lambda e: e.activation(out=out, in_=in_, func=func, **kw), r, w)

    def tt(out, in0, in1, op, r, w, eng="dve"):
        P.op(eng, lambda e: e.tensor_tensor(out=out, in0=in0, in1=in1, op=op), r, w)

    def ts(out, in0, s1, s2, op0, op1, r, w, eng="dve"):
        if s2 is None:
            P.op(eng, lambda e: e.tensor_scalar(out=out, in0=in0, scalar1=s1, scalar2=None, op0=op0), r, w)
        else:
            P.op(eng, lambda e: e.tensor_scalar(out=out, in0=in0, scalar1=s1, scalar2=s2, op0=op0, op1=op1), r, w)

    def stt(out, in0, scalar, in1, op0, op1, r, w, eng="dve"):
        P.op(eng, lambda e: e.scalar_tensor_tensor(out=out, in0=in0, scalar=scalar, in1=in1, op0=op0, op1=op1), r, w)

    def cp(out, in_, r, w, eng="dve"):
        P.op(eng, lambda e: e.tensor_copy(out=out, in_=in_), r, w)

    def recip(out, in_, r, w):
        P.op("dve", lambda e: e.reciprocal(out=out, in_=in_), r, w)

    def memset(ap, val, w, eng="dve"):
        P.op(eng, lambda e: e.memset(ap, val), (), w)

    bank_rr = [0]

    def nb(lo=0, hi=8):
        b = lo + bank_rr[0] % (hi - lo)
        bank_rr[0] += 1
        return b

    wstate = dict(next=0)

    def wload(chunk, ncols, slot=0):
        P.dma("pool", eslot[slot][:, 0:ncols], wts_d[chunk, :, 0:ncols], (), ("w%d" % slot,), "we%d" % slot)
        return slot

    def wview(slot, pat=None, **kw):
        v = eslot[slot] if isinstance(slot, int) and slot < 2 and pat is None else wslots[slot]
        if pat is None:
            return v
        return v.rearrange(pat, **kw)

    P.tag = 'P0'
    P.dma("sp", par[:], par_d[:], (), ("par",), "c0")
    P.dma("sp", cst[:], cst_d[:], (), ("cst",), "c1")
    P.dma("sp", cv[:], cvec_d[:], (), ("cv",), "c2")
    P.dma("sp", t12b, msk_d[:], (), ("tmpo",), "c3")
    cp(identb[:], ident_f, ("cst",), ("identb",))
    cp(mskb[:], t12b, ("tmpo",), ("mskb",))
    memset(R[:, 34816:52496], 1.0, ("Vm",), eng="pool")
    memset(carry[:], 0.0, ("carry",))
    act(scb[:].rearrange("p k c -> p (k c)"), cv[:], AF.Silu, ("cv",), ("scb",))
    mb = 7
    for j in range(12):
        sl = wload(C_ADA + j, 4096, slot=0)
        wv = eslot[0].rearrange("p (k n) -> p k n", k=8)
        for q in range(4):
            ch = j * 4 + q
            for k in range(8):
                mm(bank(mb)[:, ch * 2:ch * 2 + 2], wv[:, k, q * 128:(q + 1) * 128], scb[:, k, :], k == 0, k == 7,
                   ("w%d" % sl, "scb"), (bkey(mb),))
    mview = bank(mb)[:, 0:96].rearrange("p (c t) -> p c t", t=2)
    tt(modv[:, 0, :], mview[:, :, 0], par[:, PAR_BADA:PAR_BADA + 48], ALU.add, ("par",), (bkey(mb), "modv"))
    tt(modv[:, 1, :], mview[:, :, 1], par[:, PAR_BADA:PAR_BADA + 48], ALU.add, ("par",), (bkey(mb), "modv"))
    def gsv(dst, gcol, scsrc):
        stt(vec[:, dst:dst + 8], scsrc, 1.0, par[:, gcol:gcol + 8], ALU.add, ALU.mult, ("modv", "par"), ("vec",))
    gsv(V_GS1, PAR_PREG, modv[:, 0, 8:16])
    cp(vec[:, V_SH1:V_SH1 + 8], modv[:, 0, 0:8], ("modv",), ("vec",))
    gsv(V_CGS1, PAR_PREG, modv[:, 1, 8:16])
    cp(vec[:, V_CSH1:V_CSH1 + 8], modv[:, 1, 0:8], ("modv",), ("vec",))
    gsv(V_GS2, PAR_FPREG, modv[:, 0, 32:40])
    cp(vec[:, V_SH2:V_SH2 + 8], modv[:, 0, 24:32], ("modv",), ("vec",))
    tt(vec[:, V_G1PG:V_G1PG + 8], modv[:, 0, 16:24], par[:, PAR_POSTG:PAR_POSTG + 8], ALU.mult, ("modv", "par"), ("vec",))
    tt(vec[:, V_G2PG:V_G2PG + 8], modv[:, 0, 40:48], par[:, PAR_FPOSTG:PAR_FPOSTG + 8], ALU.mult, ("modv", "par"), ("vec",))
    for gi, vcol in enumerate((V_G1PG, V_G2PG)):
        for k in range(8):
            ts(dsc[:, k, :], ident_f, vec[:, vcol + k:vcol + k + 1], None, ALU.mult, None, ("cst", "vec"), ("dsc",))
        for k in range(8):
            b_ = 5 + (k // 4)
            mm(bank(b_)[:, (k % 4) * 128:(k % 4 + 1) * 128], ones_f, dsc[:, k, :], True, True, ("cst", "dsc"), (bkey(b_),))
        cp(grow[:, gi, 0:512], bank(5), (), (bkey(5), "grow"))
        cp(grow[:, gi, 512:1024], bank(6), (), (bkey(6), "grow"))
    act(small[64:65, 0:8], par[64:65, PAR_SINK:PAR_SINK + 8], AF.Exp, ("par",), ("small",))

    def prenorm(xsrc, nt, gs_col, sh_col, dst, dst_off, rkeys, wkeys):
        tb = [nb(0, 4) for _ in range(4)]
        for t in range(nt):
            s = 0
            act(xn, xsrc[:, t, :], AF.Square, rkeys, ("xn0", "ss%d" % s), accum_out=small[:, 8 + s:9 + s])
            act(small[:, 10 + s:11 + s], small[:, 8 + s:9 + s], AF.Sqrt, ("ss%d" % s,), ("rs%d" % s,), scale=1.0 / D, bias=EPS)
            recip(small[:, 12 + s:13 + s], small[:, 10 + s:11 + s], ("rs%d" % s,), ("rstd%d" % s,))
            ts(xn, xsrc[:, t, :], small[:, 12 + s:13 + s], None, ALU.mult, None,
               tuple(rkeys) + ("rstd%d" % s,), ("xn%d" % s,))
            for k in range(8):
                b_ = tb[k // 2]
                pv = bank(b_).bitcast(BF16).rearrange("p (a t n) -> p a t n", a=2, t=4)
                tr(pv[:, k % 2, t, :], xn[:, k * 128:(k + 1) * 128], ("xn%d" % s, "identb"), (bkey(b_),))
        for k in range(8):
            b_ = tb[k // 2]
            pv = bank(b_).bitcast(BF16).rearrange("p (a n) -> p a n", a=2)
            src = pv[:, k % 2, 0:nt * 128]
            if k % 2 == 0:
                act(dst[:, k, dst_off:dst_off + nt * 128], src, AF.Identity, ("vec",), (bkey(b_),) + tuple(wkeys),
                    scale=vec[:, gs_col + k:gs_col + k + 1], bias=vec[:, sh_col + k:sh_col + k + 1])
            else:
                ts(dst[:, k, dst_off:dst_off + nt * 128], src, vec[:, gs_col + k:gs_col + k + 1],
                   vec[:, sh_col + k:sh_col + k + 1], ALU.mult, ALU.add, ("vec",), (bkey(b_),) + tuple(wkeys))

    def rms_fm(srcs_banks, nfeat_chunks, n, gcol, dst_fn):
        for c, b_ in enumerate(srcs_banks):
            act(sq[:, c, 0:n], bank(b_)[:, 0:n], AF.Square, (), (bkey(b_), "sq"))
        sb_ = nb(4, 8)
        for c in range(nfeat_chunks):
            mm(bank(sb_)[:, 0:n], ones_f, sq[:, c, 0:n], c == 0, c == nfeat_chunks - 1, ("cst", "sq"), (bkey(sb_),))
        act(rs[:, 0:n], bank(sb_)[:, 0:n], AF.Sqrt, (), (bkey(sb_), "rs"), scale=1.0 / (128 * nfeat_chunks), bias=EPS)
        recip(rstd[:, 0:n], rs[:, 0:n], ("rs",), ("rstd",))
        for c, b_ in enumerate(srcs_banks):
            stt(dst_fn(c), bank(b_)[:, 0:n], par[:, gcol + c:gcol + c + 1], rstd[:, 0:n], ALU.mult, ALU.mult,
                ("par", "rstd"), (bkey(b_), "rmsdst"))

    dump("vec", vec[:], ("vec",))
    dump("grow", grow[:], ("grow",))
    dump("modv", modv[:], ("modv",))
    dump("small", small[:], ("small",))
    P.barrier()
    P.tag = 'P1'
    s_w1 = wload(C_W1, 8 * 448, slot=0)
    s_kv = wload(C_KV, 1024, slot=1)
    for ci_ in list(range(C_WQ, C_QUP)) + list(range(C_WOAB, NCHUNK)):
        P.dma("pool", wbf_d[ci_], wts_d[ci_], (), ("wbf%d" % ci_,), "wbf%d" % ci_)
    w1 = eslot[0][:, 0:8 * 448].rearrange("p (k n) -> p k n", k=8)
    wkv = eslot[1]
    groups1 = [("ctx", 0, 256)] + [("lat", g * 512, 512) for g in range(SEQ // 512)]
    for gi, (kind, t0, n) in enumerate(groups1):
        nt = n // 128
        lat = kind == "lat"
        key0 = (CTX + t0) if lat else 0
        src_d = x_d if lat else ctx_d
        P.dma("sp", xs1[:, 0:nt, :], src_d[t0:t0 + n, :].rearrange("(t p) f -> p t f", p=128), (), ("xs1",), "xs1")
        if lat:
            P.dma("sp", tab2a, tabm_d[t0 // 512], (), ("tab2a",), "tab2a")
        prenorm(xs1, nt, V_GS1 if lat else V_CGS1, V_SH1 if lat else V_CSH1, hT5, 0, ("xs1",), ("hT5",))
        if gi < 2:
            dump("hT_g%d" % gi, hT5, ("hT5",))
        def proj(c0, m):
            b_ = nb(4, 8)
            for k in range(8):
                mm(bank(b_)[0:m, 0:n], w1[:, k, c0:c0 + m], hT5[:, k, 0:n], k == 0, k == 7, ("w%d" % s_w1, "hT5"), (bkey(b_),))
            return b_
        if lat:
            bq = [proj(0, 128), proj(128, 128)]
            rms_fm(bq, 2, n, PAR_QG, lambda c: qan[:, c, t0:t0 + n])
        bkv = proj(256, 128)
        rms_fm([bkv], 1, n, PAR_KVG, lambda c: kvn[:, 0:n])
        ba = proj(320, 96)
        if lat:
            bb = proj(352, 96)
            tt(t12b[64:96, 0, 0:n], bank(ba)[64:96, 0:n], tab2a[64:96, 0, 0:n], ALU.mult, ("tab2a",), (bkey(ba), "t12a"))
            tt(t12b[64:96, 1, 0:n], bank(bb)[64:96, 0:n], tab2a[64:96, 1, 0:n], ALU.mult, ("tab2a",), (bkey(bb), "t12b"))
            for h in range(8):
                tt(KT[64:96, h, key0:key0 + n], t12b[64:96, 0, 0:n], t12b[64:96, 1, 0:n], ALU.add, ("t12a", "t12b"), ("KT",),
                   eng="dve" if h % 2 == 0 else "pool")
        else:
            cp(t12b[64:96, 0, 0:n], bank(ba)[64:96, 0:n], (), (bkey(ba), "t12a"))
            for h in range(8):
                cp(KT[64:96, h, key0:key0 + n], t12b[64:96, 0, 0:n], ("t12a",), ("KT",), eng="dve" if h % 2 == 0 else "pool")
        for h in range(8):
            b_ = nb(4, 8)
            mm(bank(b_)[0:64, 0:n], wkv[:, h * 64:(h + 1) * 64], kvn[:, 0:n], True, True, ("w%d" % s_kv, "rmsdst"), (bkey(b_),))
            if h % 2 == 0:
                act(KT[0:64, h, key0:key0 + n], bank(b_)[0:64, 0:n], AF.Identity, (), (bkey(b_), "KT"))
            else:
                cp(KT[0:64, h, key0:key0 + n], bank(b_)[0:64, 0:n], (), (bkey(b_), "KT"))
        for t in range(nt):
            b_ = nb(4, 8)
            mm(bank(b_), kvn[:, t * 128:(t + 1) * 128], wkv[:, 512:1024], True, True, ("w%d" % s_kv, "rmsdst"), (bkey(b_),))
            cp(Vm[:, key0 // 128 + t, :, 0:64], bank(b_).rearrange("p (h d) -> p h d", h=8), (), (bkey(b_), "Vm"))

    dump("KT", R[:, 0:34816], ("KT",))
    dump("Vm", R[:, 34816:52496], ("Vm",))
    dump("qan", Q[:, 0:8192], ("rmsdst",))
    P.barrier()
    P.tag = 'P2a'
    s_qup = wload(C_QUP, 3072, slot=0)
    wq = eslot[0][:, 0:3072].rearrange("p (k n) -> p k n", k=2)
    SB = (0, 1, 2, 3)
    AB = (4, 5)
    items = [(qb, h, kt) for qb in range(SEQ // QB) for h in range(8) for kt in range(NKT)]
    NI = len(items)

    def qproj(qb, h):
        c0 = qb * QB
        if h == 0:
            P.dma("sp", tab2a, tabm_d[qb], (), ("tab2a",), "tab2a")
        ba, bb = 6, 7
        for k in range(2):
            mm(bank(ba)[0:96, :], wq[:, k, h * 96:(h + 1) * 96], qan[:, k, c0:c0 + QB], k == 0, k == 1,
               ("w%d" % s_qup, "rmsdst"), (bkey(ba),))
        for k in range(2):
            mm(bank(bb)[0:96, :], wq[:, k, 768 + h * 96:768 + (h + 1) * 96], qan[:, k, c0:c0 + QB], k == 0, k == 1,
               ("w%d" % s_qup, "rmsdst"), (bkey(bb),))
        cp(qT[0:64, h, :], bank(ba)[0:64, :], (), (bkey(ba), "qT%d" % h))
        tt(t12b[64:96, 0, :], bank(ba)[64:96, :], tab2a[64:96, 0, :], ALU.mult, ("tab2a",), (bkey(ba), "t12a"))
        tt(t12b[64:96, 1, :], bank(bb)[64:96, :], tab2a[64:96, 1, :], ALU.mult, ("tab2a",), (bkey(bb), "t12b"))
        tt(qT[64:96, h, :], t12b[64:96, 0, :], t12b[64:96, 1, :], ALU.add, ("t12a", "t12b"), ("qT%d" % h,), eng="pool")

    def qk(i):
        qb, h, kt = items[i]
        b_ = SB[i % 4]
        mm(bank(b_), KT[0:96, h, kt * 128:(kt + 1) * 128], qT[0:96, h, :], True, True, ("KT", "qT%d" % h), (bkey(b_),))

    def ex_pv(i):
        qb, h, kt = items[i]
        c0 = qb * QB
        b_ = SB[i % 4]
        pslot = i % 3
        act(Pr[:, pslot, :], bank(b_), AF.Exp, (), (bkey(b_), "P%d" % pslot), scale=MLA_SCALE)
        a_ = AB[h % 2]
        mm(bank(a_)[0:65, :], Vm[:, kt, h, :], Pr[:, pslot, :], kt == 0, kt == NKT - 1, ("Vm", "P%d" % pslot), (bkey(a_),))
        if kt == NKT - 1:
            cp(oext[0:65, :], bank(a_)[0:65, :], (), (bkey(a_), "oext"))
            d_ = 6 + (h % 2)
            mm(bank(d_)[0:64, :], sel_f[0:65, :], oext[0:65, :], True, True, ("cst", "oext"), (bkey(d_),))
            recip(rden[0:64, :], bank(d_)[0:64, :], (), (bkey(d_), "rden"))
            if h % 2 == 0:
                tt(omT[0:64, h // 2, c0:c0 + QB], oext[0:64, :], rden[0:64, :], ALU.mult, ("oext", "rden"), ("omT",))
            else:
                tt(otmp[0:64, :], oext[0:64, :], rden[0:64, :], ALU.mult, ("oext", "rden"), ("otmp",))
                cp(omT[64:128, h // 2, c0:c0 + QB], otmp[0:64, :], ("otmp",), ("omT",), eng="pool")

    LEAD = 2
    qproj(0, 0)
    for i in range(-LEAD, NI):
        if i + LEAD < NI:
            qk(i + LEAD)
        if i >= 0:
            ex_pv(i)
            qb, h, kt = items[i]
            if kt == 6:
                nh = qb * 8 + h + 1
                if nh < (SEQ // QB) * 8:
                    qproj(nh // 8, nh % 8)

    dump("omT", omT[:], ("omT",))
    P.barrier()
    P.tag = 'P1b'
    memset(Q[:, 4352:8772], 1.0, ("Vs",), eng="pool")
    groups = groups1
    xs5 = R[:, 28672:36864].bitcast(F32).rearrange("p (t f) -> p t f", t=4)
    hT5b = R[:, 36864:40960].rearrange("p (k n) -> p k n", k=8)
    s_w1b = wload(C_W1B, 8 * 384, slot=0)
    w1b = eslot[0][:, 0:8 * 384].rearrange("p (k n) -> p k n", k=8)
    for gi, (kind, t0, n) in enumerate(groups):
        nt = n // 128
        lat = kind == "lat"
        key0 = (CTX + t0) if lat else 0
        src_d = x_d if lat else ctx_d
        P.dma("sp", xs5[:, 0:nt, :], src_d[t0:t0 + n, :].rearrange("(t p) f -> p t f", p=128), (), ("xs5",), "xs5")
        if lat:
            P.dma("sp", tab2a, tabs_d[t0 // 512], (), ("tab2a",), "tab2a")
        prenorm(xs5, nt, V_GS1 if lat else V_CGS1, V_SH1 if lat else V_CSH1, hT5b, 0, ("xs5",), ("hT5b",))
        ba = nb(4, 8)
        for k in range(8):
            mm(bank(ba)[:, 0:n], w1b[:, k, 0:128], hT5b[:, k, 0:n], k == 0, k == 7, ("w%d" % s_w1b, "hT5b"), (bkey(ba),))
        if lat:
            bb = nb(4, 8)
            for k in range(8):
                mm(bank(bb)[:, 0:n], w1b[:, k, 128:256], hT5b[:, k, 0:n], k == 0, k == 7, ("w%d" % s_w1b, "hT5b"), (bkey(bb),))
            tt(t12b[:, 0, 0:n], bank(ba)[:, 0:n], tab2a[:, 0, 0:n], ALU.mult, ("tab2a",), (bkey(ba), "t12a"))
            tt(t12b[:, 1, 0:n], bank(bb)[:, 0:n], tab2a[:, 1, 0:n], ALU.mult, ("tab2a",), (bkey(bb), "t12b"))
            tt(KsT[:, key0:key0 + n], t12b[:, 0, 0:n], t12b[:, 1, 0:n], ALU.add, ("t12a", "t12b"), ("KsT",), eng="pool")
        else:
            act(KsT[:, key0:key0 + n], bank(ba)[:, 0:n], AF.Identity, (), (bkey(ba), "KsT"))
        bv = nb(4, 8)
        for t in range(nt):
            for k in range(8):
                mm(bank(bv)[:, t * 128:(t + 1) * 128], hT5b[:, k, t * 128:(t + 1) * 128], w1b[:, k, 256:384], k == 0, k == 7,
                   ("w%d" % s_w1b, "hT5b"), (bkey(bv),))
        cp(Vs[:, key0 // 128:key0 // 128 + nt, :, 0:64],
           bank(bv)[:, 0:nt * 128].rearrange("p (t g d) -> p t g d", t=nt, g=2), (), (bkey(bv), "Vs"))
    P.barrier()

    dump("KsT", KsT, ("KsT",))
    dump("Vs", Q[:, 4352:8772], ("Vs",))
    RING = 5
    sched = []
    ffn_chunks = [(C_WUP + i, 4096) for i in range(11)] + [(C_WDN + i, 4096 if i < 5 else 2048) for i in range(6)]
    back_chunks = [(C_WOAB, 4096), (C_G, 4096), (C_G + 2, 4096), (C_WOAB + 1, 4096), (C_G + 1, 4096), (C_G + 3, 4096),
                   (C_WOUT, 4096), (C_WOUT + 1, 4096)]
    for b_ in range(NB2):
        if b_ >= 2:
            sched += ffn_chunks
        sched += back_chunks
    sched += ffn_chunks + ffn_chunks
    LA = 2
    wst2 = dict(issued=0, used=0)

    def wnext(chunk, ncols=None):
        u = wst2["used"]
        assert sched[u][0] == chunk, (u, sched[u], chunk)
        while wst2["issued"] < min(len(sched), u + 1 + LA):
            i_ = wst2["issued"]
            sl_ = i_ % RING
            P.dma("sp", wslots[sl_][:, 0:sched[i_][1]], wbf_d[sched[i_][0], :, 0:sched[i_][1]], ("wbf%d" % sched[i_][0],), ("w%d" % sl_,), "w%d" % sl_)
            wst2["issued"] += 1
        wst2["used"] += 1
        return u % RING

    P.dma("pool", wslots[5], wbf_d[C_WQ], ("wbf%d" % C_WQ,), ("w5",), "w5")
    P.dma("pool", wslots[6], wbf_d[C_WQP], ("wbf%d" % C_WQP,), ("w6",), "w6")
    wqa = wslots[5].rearrange("p (k n) -> p k n", k=8)
    wqb = wslots[6].rearrange("p (k n) -> p k n", k=8)
    Ue2 = mergedT[:, 0:5, :].rearrange("p k n -> p (k n)").bitcast(F32)[:, 0:2 * (TB + 2)].rearrange("p (a n) -> p a n", a=2)
    frr = [0]

    def nbf():
        frr[0] += 1
        return frr[0] % 4

    def attn_front(b):
        c0 = b * TB
        P.dma("act", xs[:, 0:NTB, :], x_d[c0:c0 + TB, :].rearrange("(t p) f -> p t f", p=128), (), ("xs",), "xs")
        tg, toff = c0 // 512, c0 % 512
        P.dma("act", tab[:, :, 0:TB], tabs_d[tg][:, :, toff:toff + TB], (), ("tab",), "tab")
        prenorm(xs, NTB, V_GS1, V_SH1, hT, 0, ("xs",), ("hT",))
        yield
        for j in range(4):
            ba = nbf()
            for k in range(8):
                mm(bank(ba)[:, 0:TB], wqa[:, k, j * 128:(j + 1) * 128], hT[:, k, 0:TB], k == 0, k == 7, ("w5", "hT"), (bkey(ba),))
            bb = nbf()
            for k in range(8):
                mm(bank(bb)[:, 0:TB], wqb[:, k, j * 128:(j + 1) * 128], hT[:, k, 0:TB], k == 0, k == 7, ("w6", "hT"), (bkey(bb),))
            tt(t12[:, 0, 0:TB], bank(ba)[:, 0:TB], tab[:, 0, 0:TB], ALU.mult, ("tab",), (bkey(ba), "t12a"))
            tt(t12[:, 1, 0:TB], bank(bb)[:, 0:TB], tab[:, 1, 0:TB], ALU.mult, ("tab",), (bkey(bb), "t12b"))
            tt(qsT[:, j, :], t12[:, 0, 0:TB], t12[:, 1, 0:TB], ALU.add, ("t12a", "t12b"), ("qsT",))
            yield
        sw = []
        for g in range(2):
            for qt in range(NTB):
                T = b * NTB + qt
                kts = [(0, 0), (1, 0)] + [(2 + T + rel, rel) for rel in (-1, 0, 1) if 0 <= T + rel < SEQ // 128]
                for ii, (kt, rel) in enumerate(kts):
                    sw.append((g, qt, kt, rel, ii == 0, ii == len(kts) - 1))

        def sqk(i):
            g, qt, kt, rel, first, last = sw[i]
            b_ = i % 2
            mm(bank(b_), KsT[g * 64:(g + 1) * 64, kt * 128:(kt + 1) * 128],
               qsT[g * 64:(g + 1) * 64, :, qt * 128:(qt + 1) * 128], True, True, ("KsT", "qsT"), (bkey(b_),))

        def sexpv(i):
            g, qt, kt, rel, first, last = sw[i]
            b_ = i % 2
            pslot = i % 3
            act(Pr[:, pslot, :], bank(b_), AF.Exp, (), (bkey(b_), "P%d" % pslot), scale=SWA_SCALE)
            if rel != 0 and kt >= 2:
                tt(Pr[:, pslot, :], Pr[:, pslot, :], mskb[:, 0 if rel < 0 else 1, :], ALU.mult, ("mskb",), ("P%d" % pslot,))
            a_ = 2
            mm(bank(a_)[0:65, :], Vs[:, kt, g, :], Pr[:, pslot, :], first, last, ("Vs", "P%d" % pslot), (bkey(a_),))
            if last:
                act(oext[0:65, :], bank(a_)[0:65, :], AF.Identity, (), (bkey(a_), "oext"))
                for j in range(4):
                    ts(oext[64:65, j * 128:(j + 1) * 128], oext[64:65, j * 128:(j + 1) * 128],
                       small[64:65, g * 4 + j:g * 4 + j + 1], None, ALU.add, None, ("small", "oext"), ("oext",))
                d_ = 3
                mm(bank(d_)[0:64, :], sel_f[0:65, :], oext[0:65, :], True, True, ("cst", "oext"), (bkey(d_),))
                recip(rden[0:64, :], bank(d_)[0:64, :], (), (bkey(d_), "rden"))
                if g == 0:
                    tt(osT[0:64, :, qt * 128:(qt + 1) * 128], oext[0:64, :].rearrange("p (j q) -> p j q", j=4),
                       rden[0:64, :].rearrange("p (j q) -> p j q", j=4), ALU.mult, ("oext", "rden"), ("osT",))
                else:
                    tt(otmp[0:64, :], oext[0:64, :], rden[0:64, :], ALU.mult, ("oext", "rden"), ("otmp",))
                    cp(osT[64:128, :, qt * 128:(qt + 1) * 128], otmp[0:64, :].rearrange("p (j q) -> p j q", j=4),
                       ("otmp",), ("osT",), eng="pool")

        for i in range(-1, len(sw)):
            if i + 1 < len(sw):
                sqk(i + 1)
            if i >= 0:
                sexpv(i)
            yield

    def post_residual(b0, xin, xout, gi, rk, wk):
        for hf in range(2):
            act(xn[:, hf * 512:(hf + 1) * 512], bank(b0 + hf), AF.Square, (), (bkey(b0 + hf), "xn0", "ssq%d" % hf),
                accum_out=small[:, 9 + 2 * hf:10 + 2 * hf])
        tt(small[:, 14:15], small[:, 9:10], small[:, 11:12], ALU.add, ("ssq0", "ssq1"), ("ssp",))
        act(small[:, 15:16], small[:, 14:15], AF.Sqrt, ("ssp",), ("rsp",), scale=1.0 / D, bias=EPS)
        recip(small[:, 14:15], small[:, 15:16], ("rsp",), ("ssp",))
        for hf in range(2):
            stt(tmpo[:, hf * 512:(hf + 1) * 512], bank(b0 + hf), small[:, 14:15], grow[:, gi, hf * 512:(hf + 1) * 512],
                ALU.mult, ALU.mult, ("ssp", "grow"), (bkey(b0 + hf), "tmpo"))
        tt(xout, tmpo, xin, ALU.add, ("tmpo",) + tuple(rk), tuple(wk))

    def attn_back(b):
        c0 = b * TB
        xsl = b % 2
        x1 = x1r[:, xsl, :, :]
        xk = "x1_%d" % xsl
        if b == 0:
            dump("qsT", qsT, ("qsT",))
            dump("osT", osT, ("osT",))
        sg_ = {}
        for fo in range(8):
            if fo % 4 == 0:
                s_oab = wnext(C_WOAB + fo // 4, 4096)
                woa = wslots[s_oab][:, 0:2048].rearrange("p (k n) -> p k n", k=4)
                wob = wslots[s_oab][:, 2048:4096].rearrange("p (k n) -> p k n", k=4)
                sg_[0] = wnext(C_G + fo // 4, 4096)
                sg_[1] = wnext(C_G + 2 + fo // 4, 4096)
            wga = wview(sg_[0], "p (k n) -> p k n", k=8)
            wgb = wview(sg_[1], "p (k n) -> p k n", k=8)
            fc = (fo % 4) * 128
            pa, pb_, pga, pgb = nb(), nb(), nb(), nb()
            for k in range(4):
                mm(bank(pa)[:, 0:TB], woa[:, k, fc:fc + 128], omT[:, k, c0:c0 + TB], k == 0, k == 3, ("w%d" % s_oab, "omT"), (bkey(pa),))
            for k in range(4):
                mm(bank(pb_)[:, 0:TB], wob[:, k, fc:fc + 128], osT[:, k, :], k == 0, k == 3, ("w%d" % s_oab, "osT"), (bkey(pb_),))
            for k in range(8):
                mm(bank(pga)[:, 0:TB], wga[:, k, fc:fc + 128], hT[:, k, 0:TB], k == 0, k == 7, ("w%d" % sg_[0], "hT"), (bkey(pga),))
            for k in range(8):
                mm(bank(pgb)[:, 0:TB], wgb[:, k, fc:fc + 128], hT[:, k, 0:TB], k == 0, k == 7, ("w%d" % sg_[1], "hT"), (bkey(pgb),))
            act(gab[:, 0, :], bank(pga)[:, 0:TB], AF.Sigmoid, ("par",), (bkey(pga), "ga"), bias=par[:, PAR_BGA + fo:PAR_BGA + fo + 1])
            act(gab[:, 1, :], bank(pgb)[:, 0:TB], AF.Sigmoid, ("par",), (bkey(pgb), "gb"), bias=par[:, PAR_BGB + fo:PAR_BGB + fo + 1])
            tt(t12[:, 0, 0:TB], bank(pa)[:, 0:TB], gab[:, 0, :], ALU.mult, ("ga",), (bkey(pa), "t12a"))
            tt(t12[:, 1, 0:TB], bank(pb_)[:, 0:TB], gab[:, 1, :], ALU.mult, ("gb",), (bkey(pb_), "t12b"))
            tt(mergedT[:, fo, :], t12[:, 0, 0:TB], t12[:, 1, 0:TB], ALU.add, ("t12a", "t12b"), ("mergedT",))
        s_o0 = wnext(C_WOUT, 4096)
        s_o1 = wnext(C_WOUT + 1, 4096)
        wo = [wview(s_o0, "p (k n) -> p k n", k=8), wview(s_o1, "p (k n) -> p k n", k=8)]
        for t in range(NTB):
            b0 = 2 * (nb() % 4)
            for hf in range(2):
                for k in range(8):
                    mm(bank(b0 + hf), mergedT[:, k, t * 128:(t + 1) * 128], wo[hf][:, k, :], k == 0, k == 7,
                       ("w%d" % (s_o0 if hf == 0 else s_o1), "mergedT"), (bkey(b0 + hf),))
            post_residual(b0, xs[:, t, :], x1[:, t, :], 0, ("xs",), (xk,))
        if b == 0:
            dump("mergedT", mergedT, ("mergedT",))
            dump("x1", x1, (xk,))
        hsl = b % 2
        prenorm(x1, NTB, V_GS2, V_SH2, hfT, hsl * TB, (xk,), ("hf%d" % hsl,))
        if hsl == 0:
            cp(hfT[:, :, 2 * TB:2 * TB + 1], hfT[:, :, 0:1], ("hf0",), ("hfx",), eng="pool")

    def ffn_block(b):
        c0 = b * TB
        xsl = b % 2
        x1 = x1r[:, xsl, :, :]
        xk = "x1_%d" % xsl
        hsl = b % 2
        w0 = hsl * TB + 1
        hkeys = ("hf%d" % hsl, "hf%d" % (1 - hsl)) if hsl == 0 else ("hf1", "hfx")
        if b == NB2 - 1:
            memset(hfT[:, :, 2 * TB:2 * TB + 1], 0.0, ("hfx",), eng="pool")
        ub = [0]
        for i in range(11):
            s_up = wnext(C_WUP + i, 4096)
            wu = wview(s_up, "p (k n) -> p k n", k=8)
            for p_ in range(2):
                c = 2 * i + p_
                ccs = (c, NCC + c)
                bks = []
                for gv in range(2):
                    col = p_ * 256 + gv * 128
                    b_ = 4 + ub[0] % 4
                    ub[0] += 1
                    bks.append(b_)
                    if b == 0:
                        for k in range(8):
                            mm(bank(b_)[:, TB:TB + 1], wu[:, k, col:col + 128], hfT[:, k, 0:1], k == 0, k == 7,
                               ("w%d" % s_up, "hf0"), (bkey(b_),))
                        cp(carry[:, ccs[gv], 1:2], bank(b_)[:, TB:TB + 1], (), (bkey(b_), "carry"))
                    for k in range(8):
                        mm(bank(b_)[:, 0:TB], wu[:, k, col:col + 128], hfT[:, k, w0:w0 + TB], k == 0, k == 7,
                           ("w%d" % s_up,) + hkeys, (bkey(b_),))
                rr = c % 2
                UeR = Ue if rr == 0 else Ue2
                tcR = tcv if rr == 0 else gab
                for gv in range(2):
                    cc = ccs[gv]
                    b_ = bks[gv]
                    uk = ("Ue%d" % gv,) if rr == 0 else ("mergedT",)
                    tk = ("tcv%d" % gv,) if rr == 0 else (("ga",) if gv == 0 else ("gb",))
                    cp(UeR[:, gv, 0:2], carry[:, cc, :], ("carry",), uk)
                    act(UeR[:, gv, 2:TB + 2], bank(b_)[:, 0:TB], AF.Identity, (), (bkey(b_),) + uk)
                    cp(carry[:, cc, :], UeR[:, gv, TB:TB + 2], uk, ("carry",), eng="pool")
                    cw = PAR_CW + cc * 3
                    act(tcR[:, gv, :], UeR[:, gv, 0:TB], AF.Identity, uk + ("par",), tk,
                        scale=par[:, cw:cw + 1], bias=par[:, PAR_CB + cc:PAR_CB + cc + 1])
                    stt(tcR[:, gv, :], UeR[:, gv, 1:TB + 1], par[:, cw + 1:cw + 2], tcR[:, gv, :], ALU.mult, ALU.add,
                        uk + ("par",) + tk, tk)
                    stt(tcR[:, gv, :], UeR[:, gv, 2:TB + 2], par[:, cw + 2:cw + 3], tcR[:, gv, :], ALU.mult, ALU.add,
                        uk + ("par",) + tk, tk)
                tk0 = "tcv0" if rr == 0 else "ga"
                tk1 = "tcv1" if rr == 0 else "gb"
                act(sg, tcR[:, 0, :], AF.Silu, (tk0,), ("sg",))
                tt(aT[:, c, :], sg, tcR[:, 1, :], ALU.mult, ("sg", tk1), ("aT",))
                yield
        if b == 0:
            dump("aT", aT, ("aT",))
            dump("hfT", hfT, ("hf0", "hf1", "hfx"))
        for c in range(NCC):
            if c % 4 == 0:
                s_dn = wnext(C_WDN + c // 4, 4096 if c < 20 else 2048)
                wd = wview(s_dn, "p (k n) -> p k n", k=4)
            for t in range(NTB):
                for hf in range(2):
                    mm(bank(4 + 2 * t + hf), aT[:, c, t * 128:(t + 1) * 128], wd[:, c % 4, hf * 512:(hf + 1) * 512], c == 0, c == NCC - 1,
                       ("w%d" % s_dn, "aT"), (bkey(4 + 2 * t + hf),))
            yield
        for t in range(NTB):
            post_residual(4 + 2 * t, x1[:, t, :], x1[:, t, :], 1, (xk,), (xk,))
            P.dma("pool", out_d[c0 + t * 128:c0 + (t + 1) * 128, :], x1[:, t, :], (xk,), (), "out%d" % (xsl * NTB + t))
            yield

    def interleave(ga_, gf_, ratio=1):
        alive_a, alive_f = ga_ is not None, gf_ is not None
        fsteps = 0
        while alive_a or alive_f:
            if alive_a:
                try:
                    next(ga_)
                except StopIteration:
                    alive_a = False
            for _ in range(ratio):
                if alive_f and not (FSAFE and alive_a and fsteps >= 44):
                    try:
                        next(gf_)
                        fsteps += 1
                    except StopIteration:
                        alive_f = False

    for b in range(NB2):
        P.tag = 'A%d' % b
        interleave(attn_front(b), ffn_block(b - 2) if b >= 2 else None, ratio=2)
        P.tag = 'B%d' % b
        attn_back(b)
    P.tag = 'F%d' % (NB2 - 2)
    interleave(None, ffn_block(NB2 - 2))
    P.tag = 'F%d' % (NB2 - 1)
    interleave(None, ffn_block(NB2 - 1))

    P.emit()
    es.close()
    return nc


def _rope_tables(rot_dim):
    rows = SEQ // 64
    row = np.repeat(np.arange(rows, dtype=np.float32), 64)
    col = np.tile(np.arange(64, dtype=np.float32), rows)
    quarter = rot_dim // 4
    inv = (10000.0 ** (-np.arange(quarter, dtype=np.float32) / quarter)).astype(np.float32)
    ar = row[:, None] * inv
    ac = col[:, None] * inv
    ang = np.concatenate([ar, ar, ac, ac], axis=-1)
    cos = np.cos(ang).astype(np.float32)
    sin = np.sin(ang).astype(np.float32)
    q = quarter
    sgn = np.concatenate([-np.ones(q), np.ones(q), -np.ones(q), np.ones(q)]).astype(np.float32)
    return cos, sin * sgn


def _perm_idx(rot_dim):
    q = rot_dim // 4
    a = np.arange(rot_dim)
    return np.concatenate([a[q:2 * q], a[0:q], a[3 * q:4 * q], a[2 * q:3 * q]])


def _chunk_k(w, ncols_pad=None):
    kk = w.shape[0] // 128
    n = w.shape[1]
    return np.ascontiguousarray(w.reshape(kk, 128, n).transpose(1, 0, 2).reshape(128, kk * n))


def _prep_shared(inp):
    f = np.float32
    w_in = inp["w_in"][0]
    wts = np.zeros((NCHUNK, 128, WSLOT), f)

    def put(ci, arr):
        wts[ci, :, :arr.shape[1]] = arr

    w_ada = inp["w_ada"][0]
    for j in range(12):
        put(C_ADA + j, _chunk_k(w_ada[:, j * 512:(j + 1) * 512]))
    s_qa, s_kva, s_kr, s_sq, s_sk, s_sv = 0, 256, 384, 416, 928, 1056
    s_gate = 1184
    pm = _perm_idx(32)
    ps_ = _perm_idx(64)
    kr = w_in[:, s_kr:s_kr + 32]
    W1 = np.concatenate([w_in[:, 0:256], w_in[:, s_kva:s_kva + 128], kr, kr[:, pm]], axis=1)
    put(C_W1, _chunk_k(W1))
    sk = w_in[:, s_sk:s_sk + 128].reshape(D, 2, 64)
    W1B = np.concatenate([sk.reshape(D, 128), sk[:, :, ps_].reshape(D, 128), w_in[:, s_sv:s_sv + 128]], axis=1)
    put(C_W1B, _chunk_k(W1B))
    sqw = w_in[:, s_sq:s_sq + 512].reshape(D, 8, 64)
    order = [0, 4, 1, 5, 2, 6, 3, 7]
    put(C_WQ, _chunk_k(sqw[:, order, :].reshape(D, 512)))
    put(C_WQP, _chunk_k(sqw[:, order, :][:, :, ps_].reshape(D, 512)))
    for i in range(4):
        put(C_G + i, _chunk_k(w_in[:, s_gate + i * 512:s_gate + (i + 1) * 512]))
    qu = inp["mla_w_q_up"][0].reshape(256, 8, 96)
    qub = qu.copy()
    qub[:, :, 64:96] = qu[:, :, 64:96][:, :, pm]
    QUP = np.concatenate([qu.reshape(256, 768), qub.reshape(256, 768)], axis=1)
    put(C_QUP, np.ascontiguousarray(QUP.reshape(2, 128, 1536).transpose(1, 0, 2).reshape(128, 3072)))
    kvu = inp["mla_w_kv_up"][0].reshape(128, 8, 128)
    put(C_KV, np.concatenate([kvu[:, :, 0:64].reshape(128, 512), kvu[:, :, 64:128].reshape(128, 512)], axis=1))
    woa = inp["mla_w_o"][0]
    woa_r = woa.reshape(4, 128, D).transpose(1, 0, 2)
    wob = inp["swa_w_o"][0].reshape(2, 4, 64, D)
    wob_r = wob.transpose(0, 2, 1, 3).reshape(128, 4, D)
    for hh in range(2):
        put(C_WOAB + hh, np.concatenate([woa_r[:, :, hh * 512:(hh + 1) * 512].reshape(128, 2048),
                                         wob_r[:, :, hh * 512:(hh + 1) * 512].reshape(128, 2048)], axis=1))
    wout = inp["w_out"][0]
    put(C_WOUT, _chunk_k(wout[:, 0:512]))
    put(C_WOUT + 1, _chunk_k(wout[:, 512:1024]))
    wup = inp["ffn_w_up"][0]
    for i in range(11):
        cols = []
        for p_ in range(2):
            c = 2 * i + p_
            cols.append(wup[:, c * 128:(c + 1) * 128])
            cols.append(wup[:, DFF + c * 128:DFF + (c + 1) * 128])
        put(C_WUP + i, _chunk_k(np.concatenate(cols, axis=1)))
    wdn = inp["ffn_w_down"][0].reshape(NCC, 128, D)
    for i in range(6):
        cs = wdn[i * 4:(i + 1) * 4]
        put(C_WDN + i, np.ascontiguousarray(cs.transpose(1, 0, 2).reshape(128, cs.shape[0] * D)))

    par = np.zeros((128, 256), f)

    def fm(v, n):
        return np.ascontiguousarray(np.asarray(v, f).reshape(n, 128).T)

    par[:, 0:8] = fm(inp["attn_pre_g"][0], 8)
    par[:, 8:16] = fm(inp["ffn_pre_g"][0], 8)
    par[:, 16:18] = fm(inp["mla_q_norm_g"][0], 2)
    par[:, 18:19] = fm(inp["mla_kv_norm_g"][0], 1)
    par[:, 19:27] = fm(inp["b_gate"][0][0:1024], 8)
    par[:, 27:35] = fm(inp["b_gate"][0][1024:2048], 8)
    cw = inp["ffn_conv_w"][0]
    cwf = cw.reshape(3, 2 * NCC, 128).transpose(2, 1, 0)
    par[:, 35:35 + 132] = cwf.reshape(128, 132)
    par[:, 167:167 + 44] = fm(inp["ffn_conv_b"][0], 44)
    par[:, 211:211 + 48] = 0
    return wts, par


def _prep(inp):
    f = np.float32
    wts, par = _prep_shared(inp)
    return wts, par


def kernel(**inputs):
    f = np.float32
    inp = {k: np.asarray(v) for k, v in inputs.items()}
    wts, par = _prep_shared(inp)
    bada = np.ascontiguousarray(inp["b_ada"][0].astype(f).reshape(48, 128).T)
    post_g = np.ascontiguousarray(inp["attn_post_g"][0].astype(f).reshape(8, 128).T)
    fpost_g = np.ascontiguousarray(inp["ffn_post_g"][0].astype(f).reshape(8, 128).T)
    par2 = np.zeros((128, 320), f)
    par2[:, 0:211] = par[:, 0:211]
    par2[:, 211:259] = bada
    par2[:, 259:267] = post_g
    par2[:, 267:275] = fpost_g
    par2[64, 275:283] = inp["swa_sink"][0].astype(f)
    cst = np.zeros((128, 320), f)
    cst[:, 0:128] = np.eye(128, dtype=f)
    cst[:, 128:256] = 1.0
    cst[64, 256:320] = 1.0
    kk = np.arange(128)[:, None]
    qq = np.arange(128)[None, :]
    msk = np.zeros((128, 2, 512), f)
    msk[:, 0, :] = np.tile((qq <= kk).astype(f), (1, 4))
    msk[:, 1, :] = np.tile((kk <= qq).astype(f), (1, 4))
    cm, sm = _rope_tables(32)
    cs_, ss_ = _rope_tables(64)
    tabm = np.zeros((SEQ // 512, 128, 2, 512), f)
    tabs = np.zeros((SEQ // 512, 128, 2, 512), f)
    for g in range(SEQ // 512):
        sl = slice(g * 512, (g + 1) * 512)
        tabm[g, 64:96, 0, :] = cm[sl].T
        tabm[g, 64:96, 1, :] = sm[sl].T
        tabs[g, 0:64, 0, :] = cs_[sl].T
        tabs[g, 64:128, 0, :] = cs_[sl].T
        tabs[g, 0:64, 1, :] = ss_[sl].T
        tabs[g, 64:128, 1, :] = ss_[sl].T
    nc = build_program()
    in_maps = []
    for b in range(8):
        cvec = np.zeros((128, 16), f)
        cvec.reshape(128, 8, 2)[:, :, 0] = inp["c"][b].astype(f).reshape(8, 128).T
        cvec.reshape(128, 8, 2)[:, :, 1] = inp["c_ctx"].astype(f).reshape(8, 128).T
        in_maps.append({
            "x": np.ascontiguousarray(inp["x"][b], dtype=f),
            "ctx": np.ascontiguousarray(inp["ctx"][b], dtype=f),
            "cvec": cvec, "wts": wts, "par": par2, "tabm": tabm, "tabs": tabs, "cst": cst, "msk": msk,
        })
    res = run_bass_kernel_spmd(nc, in_maps, core_ids=list(range(8)))
    return np.stack([np.asarray(r["out"], dtype=f) for r in res.results], axis=0)
```
